# Optimizing a Trainium2 kernel written in Bass

```python
import jax, jax.numpy as jnp
from jax import lax
import numpy as np

D_MODEL = 1024
BATCH = 16
SEQ = 4096
DEPTH = 4

N_MIXERS = 4
D_FF = 2816
RMS_EPS = 1e-6
LN_EPS = 1e-5
CONV_WIDTH = 31
FOX_HEADS = 16
FOX_HEAD_DIM = D_MODEL // FOX_HEADS
FOX_BLOCK = 128
HGRN_EXPAND = 128
HGRN_HEADS = D_MODEL // HGRN_EXPAND
HGRN_DK = HGRN_HEADS * HGRN_EXPAND
HGRN_DV = D_MODEL
HGRN_HEAD_DV = HGRN_DV // HGRN_HEADS
HGRN_CHUNK = 32
POOL_WINDOWS = (2, 4, 8, 16)
POOL_GROUP = D_MODEL // len(POOL_WINDOWS)

kernel_name = "hybrid_conv_fox_hgrn2_pool_macaron"


def rms_norm(x, g):
    x32 = x.astype(jnp.float32)
    y = x32 * lax.rsqrt(jnp.mean(x32 * x32, axis=-1, keepdims=True) + RMS_EPS)
    return (y * g.astype(jnp.float32)).astype(x.dtype)


def layer_norm(x, g, b):
    x32 = x.astype(jnp.float32)
    mu = jnp.mean(x32, axis=-1, keepdims=True)
    xc = x32 - mu
    y = xc * lax.rsqrt(jnp.mean(xc * xc, axis=-1, keepdims=True) + LN_EPS)
    return (y * g.astype(jnp.float32) + b.astype(jnp.float32)).astype(x.dtype)


def swiglu(h, w_gate, w_up, w_down):
    return (jax.nn.silu(h @ w_gate) * (h @ w_up)) @ w_down


def conv_module(h, w_in, b_in, dw, dw_b, ln_g, ln_b, w_out):
    a, b = jnp.split(h @ w_in + b_in, 2, axis=-1)
    u = a * jax.nn.sigmoid(b)
    u = lax.conv_general_dilated(
        u, dw[:, None, :], window_strides=(1,),
        padding=((CONV_WIDTH - 1, 0),),
        dimension_numbers=('NWC', 'WIO', 'NWC'),
        feature_group_count=D_MODEL) + dw_b
    u = jax.nn.silu(layer_norm(u, ln_g, ln_b))
    return u @ w_out


def fox_attention(h, w_in, b_f, w_out):
    B, S, _ = h.shape
    proj = h @ w_in
    def heads(t):
        return t.reshape(B, S, FOX_HEADS, FOX_HEAD_DIM).transpose(0, 2, 1, 3)
    q = heads(proj[..., :D_MODEL])
    k = heads(proj[..., D_MODEL:2 * D_MODEL])
    v = heads(proj[..., 2 * D_MODEL:3 * D_MODEL])
    log_f = jax.nn.log_sigmoid((proj[..., 3 * D_MODEL:] + b_f).astype(jnp.float32))
    c = jnp.cumsum(log_f, axis=1).transpose(0, 2, 1)
    scale = FOX_HEAD_DIM ** -0.5
    outs = []
    for blk in range(S // FOX_BLOCK):
        q0, q1 = blk * FOX_BLOCK, (blk + 1) * FOX_BLOCK
        logits = (jnp.einsum('bhqd,bhkd->bhqk', q[:, :, q0:q1], k[:, :, :q1]).astype(jnp.float32) * scale
                  + c[:, :, q0:q1, None] - c[:, :, None, :q1])
        causal = (q0 + jnp.arange(FOX_BLOCK))[:, None] >= jnp.arange(q1)[None, :]
        p = jax.nn.softmax(jnp.where(causal, logits, -jnp.inf), axis=-1)
        outs.append(jnp.einsum('bhqk,bhkd->bhqd', p.astype(v.dtype), v[:, :, :q1]))
    o = jnp.concatenate(outs, axis=2).transpose(0, 2, 1, 3).reshape(B, S, D_MODEL)
    return o @ w_out


def hgrn2_mixer(h, w_in, lb, norm_g, w_out):
    B, S, _ = h.shape
    proj = h @ w_in
    q = jax.nn.silu(proj[..., :HGRN_DK]).astype(jnp.float32)
    f_raw = proj[..., HGRN_DK:2 * HGRN_DK].astype(jnp.float32)
    i_in = proj[..., 2 * HGRN_DK:2 * HGRN_DK + HGRN_DV].astype(jnp.float32)
    g_out = proj[..., 2 * HGRN_DK + HGRN_DV:]
    lb = lb.astype(jnp.float32)
    log_f = jnp.logaddexp(jnp.log(lb), jnp.log1p(-lb) + jax.nn.log_sigmoid(f_raw))
    k = (1.0 - lb) * jax.nn.sigmoid(-f_raw)
    n_chunks = S // HGRN_CHUNK

    def to_chunks(t, d):
        return t.reshape(B, n_chunks, HGRN_CHUNK, HGRN_HEADS, d).transpose(1, 0, 3, 2, 4)

    qc = to_chunks(q, HGRN_EXPAND)
    kc = to_chunks(k, HGRN_EXPAND)
    gc = to_chunks(log_f, HGRN_EXPAND)
    vc = to_chunks(i_in, HGRN_HEAD_DV)
    causal = jnp.tril(jnp.ones((HGRN_CHUNK, HGRN_CHUNK), dtype=bool))[:, :, None]

    def step(state, inp):
        q_t, k_t, g_t, v_t = inp
        G = jnp.cumsum(g_t, axis=2)
        o_inter = jnp.einsum('bhtk,bhkv->bhtv', q_t * jnp.exp(G), state)
        diff = G[:, :, :, None, :] - G[:, :, None, :, :]
        decay = jnp.exp(jnp.where(causal, diff, -jnp.inf))
        A = jnp.einsum('bhtk,bhtsk,bhsk->bhts', q_t, decay, k_t)
        o = o_inter + jnp.einsum('bhts,bhsv->bhtv', A, v_t)
        G_last = G[:, :, -1:, :]
        state = (jnp.exp(G_last[:, :, 0, :])[..., None] * state
                 + jnp.einsum('bhsk,bhsv->bhkv', k_t * jnp.exp(G_last - G), v_t))
        return state, o

    state0 = jnp.zeros((B, HGRN_HEADS, HGRN_EXPAND, HGRN_HEAD_DV), jnp.float32)
    _, o = lax.scan(step, state0, (qc, kc, gc, vc))
    o = o.transpose(1, 0, 3, 2, 4).reshape(B, S, HGRN_HEADS, HGRN_HEAD_DV)
    o = o * lax.rsqrt(jnp.mean(o * o, axis=-1, keepdims=True) + RMS_EPS)
    o = o * norm_g.astype(jnp.float32).reshape(HGRN_HEADS, HGRN_HEAD_DV)
    o = o.reshape(B, S, HGRN_DV) * jax.nn.silu(g_out.astype(jnp.float32))
    return o.astype(h.dtype) @ w_out


def pool_mixer(h, w, scale):
    B, S, _ = h.shape
    h32 = h.astype(jnp.float32)
    pos = jnp.arange(1, S + 1, dtype=jnp.float32)[None, :, None]
    outs = []
    for gi, win in enumerate(POOL_WINDOWS):
        xg = h32[..., gi * POOL_GROUP:(gi + 1) * POOL_GROUP]
        cs = jnp.cumsum(xg, axis=1)
        lag = jnp.pad(cs, ((0, 0), (win, 0), (0, 0)))[:, :S]
        mean = (cs - lag) / jnp.minimum(pos, float(win))
        outs.append((mean - xg).astype(h.dtype) @ w[gi])
    return jnp.concatenate(outs, axis=-1) * scale


def setup_inputs(seed: int = 0) -> dict:
    key = jax.random.key(seed)
    ks = jax.random.split(key, 32)
    n_a, n_b, n_c, n_d = (len(range(m, DEPTH, N_MIXERS)) for m in range(N_MIXERS))
    f32 = jnp.float32

    def w(k, shape, fan_in):
        return jax.random.normal(k, shape, f32) * fan_in ** -0.5

    def gain(k, shape):
        return 1.0 + 0.02 * jax.random.normal(k, shape, f32)

    def bias(k, shape):
        return 0.02 * jax.random.normal(k, shape, f32)

    return {
        "x": jax.random.normal(ks[0], (BATCH, SEQ, D_MODEL), f32),
        "ffn_norm": gain(ks[1], (DEPTH, 2, D_MODEL)),
        "ffn_w_gate": w(ks[2], (DEPTH, 2, D_MODEL, D_FF), D_MODEL),
        "ffn_w_up": w(ks[3], (DEPTH, 2, D_MODEL, D_FF), D_MODEL),
        "ffn_w_down": w(ks[4], (DEPTH, 2, D_FF, D_MODEL), D_FF),
        "mix_norm": gain(ks[5], (DEPTH, D_MODEL)),
        "final_norm": gain(ks[6], (D_MODEL,)),
        "conv_w_in": w(ks[7], (n_a, D_MODEL, 2 * D_MODEL), D_MODEL),
        "conv_b_in": bias(ks[8], (n_a, 2 * D_MODEL)),
        "conv_dw": w(ks[9], (n_a, CONV_WIDTH, D_MODEL), CONV_WIDTH),
        "conv_dw_b": bias(ks[10], (n_a, D_MODEL)),
        "conv_ln_g": gain(ks[11], (n_a, D_MODEL)),
        "conv_ln_b": bias(ks[12], (n_a, D_MODEL)),
        "conv_w_out": w(ks[13], (n_a, D_MODEL, D_MODEL), D_MODEL),
        "fox_w_in": w(ks[14], (n_b, D_MODEL, 3 * D_MODEL + FOX_HEADS), D_MODEL),
        "fox_b_f": 1.0 + 3.0 * jax.random.uniform(ks[15], (n_b, FOX_HEADS), f32),
        "fox_w_out": w(ks[16], (n_b, D_MODEL, D_MODEL), D_MODEL),
        "hgrn_w_in": w(ks[17], (n_c, D_MODEL, 2 * HGRN_DK + 2 * HGRN_DV), D_MODEL),
        "hgrn_lb_logits": 0.1 * jax.random.normal(ks[18], (DEPTH, HGRN_DK), f32),
        "hgrn_norm": gain(ks[19], (n_c, HGRN_DV)),
        "hgrn_w_out": w(ks[20], (n_c, HGRN_DV, D_MODEL), HGRN_DV),
        "pool_w": w(ks[21], (n_d, len(POOL_WINDOWS), POOL_GROUP, POOL_GROUP), POOL_GROUP),
        "pool_scale": 1.0 + 0.1 * jax.random.normal(ks[22], (n_d, D_MODEL), f32),
    }


def reference(x, ffn_norm, ffn_w_gate, ffn_w_up, ffn_w_down, mix_norm, final_norm,
              conv_w_in, conv_b_in, conv_dw, conv_dw_b, conv_ln_g, conv_ln_b, conv_w_out,
              fox_w_in, fox_b_f, fox_w_out,
              hgrn_w_in, hgrn_lb_logits, hgrn_norm, hgrn_w_out,
              pool_w, pool_scale):
    p = jax.nn.softmax(hgrn_lb_logits.astype(jnp.float32), axis=0)
    lower_bounds = jnp.cumsum(p, axis=0) - p[0]
    for i in range(DEPTH):
        m, j = i % N_MIXERS, i // N_MIXERS
        x = x + 0.5 * swiglu(rms_norm(x, ffn_norm[i, 0]), ffn_w_gate[i, 0], ffn_w_up[i, 0], ffn_w_down[i, 0])
        h = rms_norm(x, mix_norm[i])
        if m == 0:
            y = conv_module(h, conv_w_in[j], conv_b_in[j], conv_dw[j], conv_dw_b[j],
                            conv_ln_g[j], conv_ln_b[j], conv_w_out[j])
        elif m == 1:
            y = fox_attention(h, fox_w_in[j], fox_b_f[j], fox_w_out[j])
        elif m == 2:
            y = hgrn2_mixer(h, hgrn_w_in[j], lower_bounds[i], hgrn_norm[j], hgrn_w_out[j])
        else:
            y = pool_mixer(h, pool_w[j], pool_scale[j])
        x = x + y
        x = x + 0.5 * swiglu(rms_norm(x, ffn_norm[i, 1]), ffn_w_gate[i, 1], ffn_w_up[i, 1], ffn_w_down[i, 1])
    return rms_norm(x, final_norm)
```

```python
from contextlib import ExitStack
import numpy as np
import concourse.bass as bass
import concourse.mybir as mybir
from concourse.bass_utils import run_bass_kernel_spmd

F32 = mybir.dt.float32
BF16 = mybir.dt.bfloat16
AF = mybir.ActivationFunctionType
ALU = mybir.AluOpType

D = 1024
DFF = 2816
NFF = DFF // 128
NC = 8
TT = 512
RMS_EPS = 1e-6
LN_EPS = 1e-5
NCORES = 8

_UID = [0]


def _u(name):
    _UID[0] += 1
    return "%s_%d" % (name, _UID[0])


_BLK = dict(pe="tensor", act="scalar", dve="vector", pool="gpsimd", sp="sync")


class Reg:
    __slots__ = ("w", "r")

    def __init__(self):
        self.w = {}
        self.r = {}


def _merge(d, tok):
    if tok is None:
        return
    k = tok[0]
    if k not in d or d[k][2] < tok[2]:
        d[k] = tok


class Prog:
    ENG = ["pe", "act", "dve", "pool", "sp"]

    def __init__(self, nc):
        self.nc = nc
        self.q = {k: [] for k in self.ENG}
        self.sem = {k: nc.alloc_semaphore(name="sem_" + k) for k in self.ENG}
        self.cnt = {k: 0 for k in self.ENG}
        self.seen = {k: {} for k in self.ENG}
        self.dsem = {}
        self.last = {}

    def _waits(self, eng, deps):
        ws = []
        for d in deps:
            if d is None:
                continue
            key, sem, val = d
            if key == "pe" and eng == "pe":
                continue
            if self.seen[eng].get(key, 0) < val:
                self.seen[eng][key] = val
                ws.append((sem, val))
        return ws

    def op(self, eng, fn, deps=(), sig=True):
        ws = self._waits(eng, deps)
        tok = None
        inc = None
        if sig:
            self.cnt[eng] += 1
            tok = (eng, self.sem[eng], self.cnt[eng])
            inc = (self.sem[eng], 1)
            self.last[eng] = tok
        self.q[eng].append((ws, fn, inc))
        return tok

    def dma(self, eng, slot, out, in_, deps=(), **kw):
        if slot not in self.dsem:
            self.dsem[slot] = [self.nc.alloc_semaphore(name="d_" + slot), 0]
        s = self.dsem[slot]
        s[1] += 16
        ws = self._waits(eng, deps)
        self.q[eng].append((ws, lambda e: e.dma_start(out=out, in_=in_, **kw), (s[0], 16)))
        tok = ("d_" + slot, s[0], s[1])
        self.last["d_" + slot] = tok
        return tok

    @staticmethod
    def _deps(eng, reads, writes, extra, nowaw=False):
        deps = [t for t in extra if t is not None]
        for R in reads:
            deps += list(R.w.values())
        for R in writes:
            for t in R.w.values():
                if nowaw and t[0] == eng:
                    continue
                deps.append(t)
            deps += list(R.r.values())
        return deps

    @staticmethod
    def _mark(tok, reads, writes):
        for R in reads:
            _merge(R.r, tok)
        for R in writes:
            R.w = {tok[0]: tok}
            R.r = {}

    def do(self, eng, fn, reads=(), writes=(), extra=(), nowaw=False):
        tok = self.op(eng, fn, self._deps(eng, reads, writes, extra, nowaw))
        self._mark(tok, reads, writes)
        return tok

    def mm(self, out_ap, out_reg, items, reads, extra=(), tp=None):
        deps = self._deps("pe", reads, [out_reg], extra)
        n = len(items)
        tok = None
        for i, (l, r) in enumerate(items):
            tok = self.op("pe", lambda e, l=l, r=r, i=i: e.matmul(out_ap, lhsT=l, rhs=r, start=(i == 0),
                                                                   stop=(i == n - 1)),
                          deps if i == 0 else (), sig=(i == n - 1))
        self._mark(tok, reads, [out_reg])
        return tok

    def ld(self, eng, slot, pairs, writes, reads=(), extra=(), **kw):
        deps = self._deps("d_" + slot, reads, writes, extra)
        tok = None
        for i, (o, s) in enumerate(pairs):
            tok = self.dma(eng, slot, o, s, deps if i == 0 else (), **kw)
        self._mark(tok, reads, writes)
        return tok

    def barrier(self):
        toks = list(self.last.values())
        for k in self.ENG:
            ws = self._waits(k, toks)
            if ws:
                self.q[k].append((ws, None, None))

    def emit(self):
        with self.nc.Block() as block:
            for k in self.ENG:
                if not self.q[k]:
                    continue

                def body(e, k=k):
                    for ws, fn, inc in self.q[k]:
                        for sem, val in ws:
                            e.wait_ge(sem, val)
                        if fn is None:
                            continue
                        ins = fn(e)
                        if inc is not None:
                            ins.then_inc(inc[0], inc[1])

                getattr(block, _BLK[k])(body)


PAR = {}


def _par_layout():
    off = 0
    lay = {}

    def add(name, ncol):
        nonlocal off
        lay[name] = (off, ncol)
        off += ncol

    for i in range(4):
        for k in range(2):
            add("ffn_norm_%d_%d" % (i, k), 8)
        add("mix_norm_%d" % i, 8)
    add("final_norm", 8)
    add("conv_b_in", 16)
    add("conv_dw", 31 * 8)
    add("conv_dw_b", 8)
    add("conv_ln_g", 8)
    add("conv_ln_b", 8)
    add("hgrn_lb", 32)
    add("hgrn_norm", 8)
    add("pool_scale", 8)
    add("fox_b_f", 1)
    lay["_n"] = (off, 0)
    return lay


PAR = _par_layout()
NPAR = PAR["_n"][0]


def _pcol(v):
    v = np.asarray(v, dtype=np.float32).reshape(-1, 128)
    return np.ascontiguousarray(v.T)


def pack_params(inp):
    par = np.zeros((128, NPAR), np.float32)

    def put(name, arr):
        o, n = PAR[name]
        par[:, o:o + n] = arr

    for i in range(4):
        for k in range(2):
            put("ffn_norm_%d_%d" % (i, k), _pcol(inp["ffn_norm"][i, k]))
        put("mix_norm_%d" % i, _pcol(inp["mix_norm"][i]))
    put("final_norm", _pcol(inp["final_norm"]))
    put("conv_b_in", _pcol(inp["conv_b_in"][0]))
    dw = np.asarray(inp["conv_dw"][0], np.float32)
    put("conv_dw", np.ascontiguousarray(dw.reshape(31, 8, 128).transpose(2, 0, 1).reshape(128, 248)))
    put("conv_dw_b", _pcol(inp["conv_dw_b"][0]))
    put("conv_ln_g", _pcol(inp["conv_ln_g"][0]))
    put("conv_ln_b", _pcol(inp["conv_ln_b"][0]))
    lbl = np.asarray(inp["hgrn_lb_logits"], np.float32)
    put("hgrn_lb", np.ascontiguousarray(lbl.reshape(4, 8, 128).transpose(2, 0, 1).reshape(128, 32)))
    put("hgrn_norm", _pcol(inp["hgrn_norm"][0]))
    put("pool_scale", _pcol(inp["pool_scale"][0]))
    bf = np.zeros((128, 1), np.float32)
    bf[:16, 0] = np.asarray(inp["fox_b_f"][0], np.float32)
    put("fox_b_f", bf)
    return par


class Ctx:
    def __init__(self, nc, P, NT, SEQ):
        self.nc, self.P, self.NT, self.SEQ = nc, P, NT, SEQ
        self.par = None
        self.ps = None
        self.psR = None

    def pc(self, name, c0=0, n=None):
        o, m = PAR[name]
        if n is None:
            n = m - c0
        return self.par[:, o + c0:o + c0 + n]


def mkps(cx, es, n=7):
    ps = [es.enter_context(cx.nc.psum_tensor(_u("ps%d" % i), [128, TT], F32)) for i in range(n)]
    psR = [Reg() for _ in range(n)]
    cx.ps, cx.psR = ps, psR
    return ps, psR


def _wsplit(n):
    for d in (2048, 1544, 1408, 1024, 512, 256, 128):
        if n % d == 0 and d <= 2048:
            return d
    return n


def load_w(cx, es, name, w2d, K, N):
    nc, P = cx.nc, cx.P
    kc = K // 128
    W = es.enter_context(nc.sbuf_tensor(_u(name), [128, kc, N], BF16))
    R = Reg()
    v = w2d.rearrange("(c p) f -> p c f", p=128)
    P.ld("pool", name, [(W[:, c, :], v[:, c, :]) for c in range(kc)], [R], max_dma_last_dim=_wsplit(N) * 4)
    return W, R


class NormBufs:
    def __init__(self, cx, es):
        nc = cx.nc
        self.sq = es.enter_context(nc.sbuf_tensor(_u("nb_sq"), [128, NC, TT], BF16))
        self.rstd = es.enter_context(nc.sbuf_tensor(_u("nb_rstd"), [128, TT], F32))
        self.ones = es.enter_context(nc.sbuf_tensor(_u("nb_ones"), [128, 128], BF16))
        self.eps = es.enter_context(nc.sbuf_tensor(_u("nb_eps"), [128, 1], F32))
        self.sqR, self.rstdR, self.onesR = Reg(), Reg(), Reg()
        cx.P.do("pool", lambda e: e.memset(self.ones[:], 1.0), writes=[self.onesR])
        cx.P.do("pool", lambda e: e.memset(self.eps[:], RMS_EPS), writes=[self.onesR])


def rms_sq(cx, nb, xt, xtR):
    cx.P.do("act", lambda e: e.activation(out=nb.sq[:], in_=xt[:], func=AF.Square), reads=[xtR], writes=[nb.sqR])


def rms_rstd(cx, nb, xt, xtR, psS, psSR, skip_sq=False):
    P = cx.P
    if not skip_sq:
        rms_sq(cx, nb, xt, xtR)
    P.mm(psS[:], psSR, [(nb.ones[:], nb.sq[:, c, :]) for c in range(NC)], reads=[nb.sqR, nb.onesR])
    P.do("act", lambda e: e.activation(out=nb.rstd[:], in_=psS[:], func=AF.Ln, bias=nb.eps[:], scale=1.0 / D),
         reads=[psSR], writes=[nb.rstdR])
    P.do("act", lambda e: e.activation(out=nb.rstd[:], in_=nb.rstd[:], func=AF.Exp, scale=-0.5),
         reads=[nb.rstdR], writes=[nb.rstdR])


def rms_apply(cx, nb, xt, xtR, gcol, out3, outR, off=0):
    P = cx.P
    for c in range(NC):
        P.do("dve", lambda e, c=c: e.scalar_tensor_tensor(
            out=out3[:, c, off:off + TT], in0=xt[:, c, :], scalar=gcol[:, c:c + 1], in1=nb.rstd[:],
            op0=ALU.mult, op1=ALU.mult), reads=[xtR, nb.rstdR], writes=[outR], nowaw=True)


def xview(ap):
    return ap.rearrange("(c p) t -> p c t", p=128)


def ffn_phase(cx, src, dst, wg, wu, wd, gcol, final_g=None):
    nc, P, NT = cx.nc, cx.P, cx.NT
    ntile = NT // TT
    with ExitStack() as es:
        def sb(name, shape, dt):
            return es.enter_context(nc.sbuf_tensor(_u(name), shape, dt))

        mkps(cx, es)
        Wg = sb("Wg", [128, NC, DFF], BF16)
        Wu = sb("Wu", [128, NC, DFF], BF16)
        fblk = [(0, 6), (6, 12), (12, 17), (17, 22)]
        WgR = [Reg() for _ in fblk]
        WuR = [Reg() for _ in fblk]
        wgv = wg.rearrange("(c p) f -> p c f", p=128)
        wuv = wu.rearrange("(c p) f -> p c f", p=128)
        for bi, (f0, f1) in enumerate(fblk):
            cs_ = slice(f0 * 128, f1 * 128)
            P.ld("pool", "wg%d" % bi, [(Wg[:, :, cs_], wgv[:, :, cs_])], [WgR[bi]])
            P.ld("pool", "wu%d" % bi, [(Wu[:, :, cs_], wuv[:, :, cs_])], [WuR[bi]])
        blk_of = {}
        for bi, (f0, f1) in enumerate(fblk):
            for f in range(f0, f1):
                blk_of[f] = bi
        Wd, WdR = load_w(cx, es, "Wd", wd, DFF, D)
        xt = [sb("xt%d" % i, [128, NC, TT], F32) for i in range(2)]
        xtR = [Reg(), Reg()]
        nb = NormBufs(cx, es)
        hT = sb("hT", [128, NC, TT], BF16)
        hTR = Reg()
        hid = sb("hid", [128, NFF, TT], BF16)
        hidR = [Reg() for _ in range(NFF)]
        psS, psSR = cx.ps[0], cx.psR[0]
        psG, psGR = cx.ps[1:3], cx.psR[1:3]
        psU, psUR = cx.ps[3:5], cx.psR[3:5]
        psD, psDR = cx.ps[5:7], cx.psR[5:7]
        srcv, dstv = xview(src), xview(dst)

        def x_load(j):
            b = j % 2
            P.ld("sp", "x%d" % b, [(xt[b][:], srcv[:, :, j * TT:(j + 1) * TT])], [xtR[b]])

        def norm_b(j):
            b = j % 2
            rms_rstd(cx, nb, xt[b], xtR[b], psS, psSR, skip_sq=True)
            rms_apply(cx, nb, xt[b], xtR[b], gcol, hT, hTR)

        def fin_b(j):
            b = j % 2
            rms_rstd(cx, nb, xt[b], xtR[b], psS, psSR, skip_sq=True)
            rms_apply(cx, nb, xt[b], xtR[b], final_g, xt[b], xtR[b])
            P.ld("sp", "o%d" % b, [(dstv[:, :, j * TT:(j + 1) * TT], xt[b][:])], [], reads=[xtR[b]])

        def gu_stage(j, hooks):
            for f in range(NFF):
                for hk in hooks.get(f, ()):
                    hk()
                pb = f % 2
                fs = slice(f * 128, (f + 1) * 128)
                P.mm(psG[pb][:], psGR[pb], [(Wg[:, c, fs], hT[:, c, :]) for c in range(NC)],
                     reads=[hTR, WgR[blk_of[f]]])
                P.mm(psU[pb][:], psUR[pb], [(Wu[:, c, fs], hT[:, c, :]) for c in range(NC)],
                     reads=[hTR, WuR[blk_of[f]]])
                P.do("act", lambda e, f=f, pb=pb: e.activation(out=hid[:, f, :], in_=psG[pb][:], func=AF.Silu),
                     reads=[psGR[pb]], writes=[hidR[f]])
                P.do("dve", lambda e, f=f, pb=pb: e.tensor_tensor(out=hid[:, f, :], in0=hid[:, f, :],
                                                                  in1=psU[pb][:], op=ALU.mult),
                     reads=[psUR[pb], hidR[f]], writes=[hidR[f]])

        def down_stage(j):
            b = j % 2
            sl = slice(j * TT, (j + 1) * TT)
            for c in range(NC):
                pb = c % 2
                cs = slice(c * 128, (c + 1) * 128)
                P.mm(psD[pb][:], psDR[pb], [(Wd[:, f, cs], hid[:, f, :]) for f in range(NFF)],
                     reads=hidR + [WdR])
                P.do("dve", lambda e, c=c, pb=pb: e.scalar_tensor_tensor(
                    out=xt[b][:, c, :], in0=psD[pb][:], scalar=0.5, in1=xt[b][:, c, :],
                    op0=ALU.mult, op1=ALU.add), reads=[psDR[pb], xtR[b]], writes=[xtR[b]], nowaw=True)
            if final_g is None:
                P.ld("sp", "o%d" % b, [(dstv[:, :, sl], xt[b][:])], [], reads=[xtR[b]])

        x_load(0)
        rms_sq(cx, nb, xt[0], xtR[0])
        norm_b(0)
        for j in range(ntile):
            hooks = {}
            fin = final_g is not None and j >= 1
            if fin:
                hooks[2] = [lambda j=j: rms_sq(cx, nb, xt[(j - 1) % 2], xtR[(j - 1) % 2])]
                hooks[8] = [lambda j=j: fin_b(j - 1)]
            if j + 1 < ntile:
                hooks.setdefault(9 if fin else 0, []).append(lambda j=j: x_load(j + 1))
                hooks.setdefault(14, []).append(lambda j=j: rms_sq(cx, nb, xt[(j + 1) % 2], xtR[(j + 1) % 2]))
            gu_stage(j, hooks)
            if j + 1 < ntile:
                norm_b(j + 1)
            down_stage(j)
        if final_g is not None:
            rms_sq(cx, nb, xt[(ntile - 1) % 2], xtR[(ntile - 1) % 2])
            fin_b(ntile - 1)
        P.barrier()


def final_phase(cx, src, dst, gcol):
    nc, P, NT = cx.nc, cx.P, cx.NT
    with ExitStack() as es:
        xt = [es.enter_context(nc.sbuf_tensor(_u("xt%d" % i), [128, NC, TT], F32)) for i in range(2)]
        xtR = [Reg(), Reg()]
        nb = NormBufs(cx, es)
        mkps(cx, es, 1)
        srcv, dstv = xview(src), xview(dst)
        for j in range(NT // TT):
            b = j % 2
            sl = slice(j * TT, (j + 1) * TT)
            P.ld("sp", "x%d" % b, [(xt[b][:], srcv[:, :, sl])], [xtR[b]])
            rms_rstd(cx, nb, xt[b], xtR[b], cx.ps[0], cx.psR[0])
            rms_apply(cx, nb, xt[b], xtR[b], gcol, xt[b], xtR[b])
            P.ld("sp", "o%d" % b, [(dstv[:, :, sl], xt[b][:])], [], reads=[xtR[b]])
        P.barrier()


def copy_phase(cx, src, dst):
    nc, P, NT = cx.nc, cx.P, cx.NT
    with ExitStack() as es:
        xt = [es.enter_context(nc.sbuf_tensor(_u("xt%d" % i), [128, NC, TT], F32)) for i in range(2)]
        xtR = [Reg(), Reg()]
        srcv, dstv = xview(src), xview(dst)
        for j in range(NT // TT):
            b = j % 2
            sl = slice(j * TT, (j + 1) * TT)
            P.ld("sp", "x%d" % b, [(xt[b][:], srcv[:, :, sl])], [xtR[b]])
            P.ld("sp", "o%d" % b, [(dstv[:, :, sl], xt[b][:])], [], reads=[xtR[b]])
        P.barrier()


def pool_phase(cx, xs, pool_w):
    nc, P, NT, SEQ = cx.nc, cx.P, cx.NT, cx.SEQ
    H = 16
    with ExitStack() as es:
        def sb(name, shape, dt):
            return es.enter_context(nc.sbuf_tensor(_u(name), shape, dt))

        pps, ppsR = mkps(cx, es, 3)
        Wp = sb("Wp", [128, 8, 256], BF16)
        WpR = Reg()
        wv = pool_w.rearrange("g (k p) n -> p (g k) n", p=128)
        P.ld("pool", "Wp", [(Wp[:, i, :], wv[:, i, :]) for i in range(8)], [WpR])
        xt2 = [sb("xt%d" % i, [128, NC, TT], F32) for i in range(2)]
        xt2R = [Reg(), Reg()]
        nb = NormBufs(cx, es)
        hf = sb("hf", [128, NC, H + TT], F32)
        hfR = Reg()
        sA = sb("sA", [128, 2, H + TT], F32)
        sB = sb("sB", [128, 2, H + TT], F32)
        sAR, sBR = Reg(), Reg()
        mT = sb("mT", [128, NC, TT], BF16)
        mTR = Reg()
        rc = sb("rc", [128, 4, H], F32)
        rcR = Reg()
        tmp = sb("ptmp", [128, H], F32)
        tmpR = Reg()
        for gi in range(4):
            P.do("pool", lambda e, gi=gi: e.iota(rc[:, gi, :], [[1, H]], base=1, channel_multiplier=0,
                                                 allow_small_or_imprecise_dtypes=True), writes=[rcR])
        for gi in range(4):
            P.do("dve", lambda e, gi=gi: e.tensor_scalar(out=rc[:, gi, :], in0=rc[:, gi, :],
                                                         scalar1=float(2 ** (gi + 1)), scalar2=None, op0=ALU.min),
                 reads=[rcR], writes=[rcR])
        P.do("dve", lambda e: e.reciprocal(out=rc[:], in_=rc[:]), reads=[rcR], writes=[rcR])
        xv = xview(xs)
        psS, psSR = pps[0], ppsR[0]
        tps = SEQ // TT
        ntile = NT // TT

        def pre(j):
            bb = j % 2
            P.ld("sp", "x%d" % bb, [(xt2[bb][:], xv[:, :, j * TT:(j + 1) * TT])], [xt2R[bb]])
            rms_rstd(cx, nb, xt2[bb], xt2R[bb], psS, psSR)

        pre(0)
        for j in range(ntile):
            sl = slice(j * TT, (j + 1) * TT)
            first = (j % tps == 0)
            xt, xtR = xt2[j % 2], xt2R[j % 2]
            if first:
                P.do("pool", lambda e: e.memset(hf[:, :, 0:H], 0.0), writes=[hfR])
            else:
                P.do("act", lambda e: e.activation(out=hf[:, :, 0:H], in_=hf[:, :, TT:TT + H], func=AF.Copy),
                     reads=[hfR], writes=[hfR])
            rms_apply(cx, nb, xt, xtR, cx.pc("mix_norm_3"), hf, hfR, off=H)
            for gi in range(4):
                w = 2 ** (gi + 1)
                cur, curR = hf[:, 2 * gi:2 * gi + 2, :], hfR
                L = H + TT
                sh = 1
                lo = 0
                bufs = [(sA, sAR), (sB, sBR)]
                bi = 0
                while sh < w:
                    o, oR = bufs[bi]
                    lo2 = lo + sh
                    P.do("dve", lambda e, o=o, cur=cur, lo2=lo2, sh=sh, L=L: e.tensor_tensor(
                        out=o[:, :, lo2:L], in0=cur[:, :, lo2:L], in1=cur[:, :, lo2 - sh:L - sh], op=ALU.add),
                        reads=[curR], writes=[oR])
                    cur, curR = o[:, :, :], oR
                    lo = lo2
                    sh *= 2
                    bi ^= 1
                P.do("dve", lambda e, cur=cur, gi=gi, w=w: e.scalar_tensor_tensor(
                    out=mT[:, 2 * gi:2 * gi + 2, :], in0=cur[:, :, H:H + TT], scalar=1.0 / w,
                    in1=hf[:, 2 * gi:2 * gi + 2, H:H + TT], op0=ALU.mult, op1=ALU.subtract),
                    reads=[curR, hfR], writes=[mTR], nowaw=True)
                if first:
                    for k in range(2):
                        P.do("dve", lambda e, cur=cur, gi=gi, k=k: e.tensor_tensor(
                            out=tmp[:], in0=cur[:, k, H:2 * H], in1=rc[:, gi, :], op=ALU.mult),
                            reads=[curR, rcR], writes=[tmpR])
                        P.do("dve", lambda e, gi=gi, k=k: e.tensor_tensor(
                            out=mT[:, 2 * gi + k, 0:H], in0=tmp[:], in1=hf[:, 2 * gi + k, H:2 * H],
                            op=ALU.subtract), reads=[tmpR, hfR, mTR], writes=[mTR])
            if j + 1 < ntile:
                pre(j + 1)
            for oc in range(NC):
                gi, nh = oc // 2, oc % 2
                pb = 1 + oc % 2
                P.mm(pps[pb][:], ppsR[pb],
                     [(Wp[:, 2 * gi + k, nh * 128:(nh + 1) * 128], mT[:, 2 * gi + k, :]) for k in range(2)],
                     reads=[WpR, mTR])
                P.do("dve", lambda e, oc=oc, pb=pb, xt=xt: e.scalar_tensor_tensor(
                    out=xt[:, oc, :], in0=pps[pb][:], scalar=cx.pc("pool_scale", oc, 1), in1=xt[:, oc, :],
                    op0=ALU.mult, op1=ALU.add), reads=[ppsR[pb], xtR], writes=[xtR], nowaw=True)
            P.ld("sp", "o%d" % (j % 2), [(xv[:, :, sl], xt[:])], [], reads=[xtR])
        P.barrier()


def conv_phase(cx, xs, w_in, w_out):
    nc, P, NT, SEQ = cx.nc, cx.P, cx.NT, cx.SEQ
    H = 30
    KW = 31
    with ExitStack() as es:
        def sb(name, shape, dt):
            return es.enter_context(nc.sbuf_tensor(_u(name), shape, dt))

        ps, psR = mkps(cx, es)
        Win, WinR = load_w(cx, es, "cWin", w_in, D, 2 * D)
        Wout, WoutR = load_w(cx, es, "cWout", w_out, D, D)
        xt = [sb("xt%d" % i, [128, NC, TT], F32) for i in range(2)]
        xtR = [Reg(), Reg()]
        nb = NormBufs(cx, es)
        hT = sb("hT", [128, NC, TT], BF16)
        hTR = Reg()
        ident = sb("ident", [128, 128], F32)
        identR = Reg()
        onesf = sb("onesf", [128, 128], F32)
        Dg = sb("Dg", [128, NC * KW, 128], BF16)
        DgR = Reg()
        u = sb("u", [128, NC, H + TT], BF16)
        uR = Reg()
        sg = sb("sg", [128, TT], F32)
        sgR = Reg()
        v = sb("v", [128, NC, TT], F32)
        vR = Reg()
        zT = sb("zT", [128, NC, TT], BF16)
        zTR = Reg()
        mean = sb("mean", [128, TT], F32)
        meanR = Reg()
        var = sb("var", [128, TT], F32)
        varR = Reg()
        lneps = sb("lneps", [128, 1], F32)
        P.do("pool", lambda e: e.memset(lneps[:], LN_EPS), writes=[identR])
        P.do("pool", lambda e: e.memset(onesf[:], 1.0), writes=[identR])
        P.do("pool", lambda e: e.memset(ident[:], 1.0), writes=[identR])
        P.do("pool", lambda e: e.affine_select(out=ident[:], in_=ident[:], pattern=[[1, 128]],
                                               compare_op=ALU.is_equal, fill=0.0, base=0, channel_multiplier=-1),
             reads=[identR], writes=[identR])
        for k in range(KW):
            for c in range(NC):
                P.do("dve", lambda e, k=k, c=c: e.tensor_scalar(
                    out=Dg[:, c * KW + k, :], in0=ident[:], scalar1=cx.pc("conv_dw", k * 8 + c, 1), scalar2=None,
                    op0=ALU.mult), reads=[identR], writes=[DgR], nowaw=True)
        xv = xview(xs)
        tps = SEQ // TT
        ntile = NT // TT

        def pre(j):
            b = j % 2
            sl = slice(j * TT, (j + 1) * TT)
            P.ld("sp", "x%d" % b, [(xt[b][:], xv[:, :, sl])], [xtR[b]])
            rms_rstd(cx, nb, xt[b], xtR[b], ps[0], psR[0])
            rms_apply(cx, nb, xt[b], xtR[b], cx.pc("mix_norm_0"), hT, hTR)

        def inproj(j):
            first = (j % tps == 0)
            if first:
                P.do("pool", lambda e: e.memset(u[:, :, 0:H], 0.0), writes=[uR])
            else:
                P.do("act", lambda e: e.activation(out=u[:, :, 0:H], in_=u[:, :, TT:TT + H], func=AF.Copy),
                     reads=[uR], writes=[uR])
            for oc in range(NC):
                pa, pbk = 1 + (oc % 2), 3 + (oc % 2)
                P.mm(ps[pa][:], psR[pa], [(Win[:, c, oc * 128:(oc + 1) * 128], hT[:, c, :]) for c in range(NC)],
                     reads=[WinR, hTR])
                P.mm(ps[pbk][:], psR[pbk],
                     [(Win[:, c, D + oc * 128:D + (oc + 1) * 128], hT[:, c, :]) for c in range(NC)],
                     reads=[WinR, hTR])
                P.do("act", lambda e, oc=oc, pbk=pbk: e.activation(
                    out=sg[:], in_=ps[pbk][:], func=AF.Sigmoid, bias=cx.pc("conv_b_in", 8 + oc, 1)),
                    reads=[psR[pbk]], writes=[sgR])
                P.do("dve", lambda e, oc=oc, pa=pa: e.scalar_tensor_tensor(
                    out=u[:, oc, H:H + TT], in0=ps[pa][:], scalar=cx.pc("conv_b_in", oc, 1), in1=sg[:],
                    op0=ALU.add, op1=ALU.mult), reads=[psR[pa], sgR], writes=[uR], nowaw=True)

        def conv(j):
            for c in range(NC):
                pc_ = 5 + (c % 2)
                P.mm(ps[pc_][:], psR[pc_], [(Dg[:, c * KW + k, :], u[:, c, k:k + TT]) for k in range(KW)],
                     reads=[DgR, uR])
                P.do("act", lambda e, c=c, pc_=pc_: e.activation(
                    out=v[:, c, :], in_=ps[pc_][:], func=AF.Identity, bias=cx.pc("conv_dw_b", c, 1)),
                    reads=[psR[pc_]], writes=[vR], nowaw=True)
                P.do("act", lambda e, c=c, pc_=pc_: e.activation(
                    out=nb.sq[:, c, :], in_=ps[pc_][:], func=AF.Square, bias=cx.pc("conv_dw_b", c, 1)),
                    reads=[psR[pc_]], writes=[nb.sqR], nowaw=True)

        def rest_a(j):
            P.mm(ps[1][:], psR[1], [(onesf[:], v[:, c, :]) for c in range(NC)], reads=[vR, identR])
            P.mm(ps[2][:], psR[2], [(nb.ones[:], nb.sq[:, c, :]) for c in range(NC)], reads=[nb.sqR, nb.onesR])
            P.do("act", lambda e: e.activation(out=mean[:], in_=ps[1][:], func=AF.Copy, scale=1.0 / D),
                 reads=[psR[1]], writes=[meanR])
            P.do("dve", lambda e: e.tensor_tensor(out=var[:], in0=mean[:], in1=mean[:], op=ALU.mult),
                 reads=[meanR], writes=[varR])
            P.do("dve", lambda e: e.scalar_tensor_tensor(out=var[:], in0=ps[2][:], scalar=1.0 / D, in1=var[:],
                                                         op0=ALU.mult, op1=ALU.subtract),
                 reads=[psR[2], varR], writes=[varR])
            P.do("act", lambda e: e.activation(out=var[:], in_=var[:], func=AF.Ln, bias=lneps[:], scale=1.0),
                 reads=[varR, identR], writes=[varR])
            P.do("act", lambda e: e.activation(out=var[:], in_=var[:], func=AF.Exp, scale=-0.5),
                 reads=[varR], writes=[varR])
            for c in range(NC):
                P.do("dve", lambda e, c=c: e.tensor_tensor(out=v[:, c, :], in0=v[:, c, :], in1=mean[:],
                                                           op=ALU.subtract),
                     reads=[vR, meanR], writes=[vR], nowaw=True)
                P.do("dve", lambda e, c=c: e.tensor_tensor(out=v[:, c, :], in0=v[:, c, :], in1=var[:],
                                                           op=ALU.mult),
                     reads=[vR, varR], writes=[vR])
                P.do("act", lambda e, c=c: e.activation(
                    out=zT[:, c, :], in_=v[:, c, :], func=AF.Silu, scale=cx.pc("conv_ln_g", c, 1),
                    bias=cx.pc("conv_ln_b", c, 1)), reads=[vR], writes=[zTR], nowaw=True)

        def rest_b(j):
            b = j % 2
            sl = slice(j * TT, (j + 1) * TT)
            for oc in range(NC):
                pa = 3 + (oc % 2)
                P.mm(ps[pa][:], psR[pa], [(Wout[:, c, oc * 128:(oc + 1) * 128], zT[:, c, :]) for c in range(NC)],
                     reads=[WoutR, zTR])
                P.do("dve", lambda e, oc=oc, pa=pa: e.tensor_tensor(
                    out=xt[b][:, oc, :], in0=ps[pa][:], in1=xt[b][:, oc, :], op=ALU.add),
                    reads=[psR[pa], xtR[b]], writes=[xtR[b]], nowaw=True)
            P.ld("sp", "o%d" % b, [(xv[:, :, sl], xt[b][:])], [], reads=[xtR[b]])

        pre(0)
        inproj(0)
        for j in range(ntile):
            if j + 1 < ntile:
                pre(j + 1)
            conv(j)
            rest_a(j)
            if j + 1 < ntile:
                inproj(j + 1)
            rest_b(j)
        P.barrier()


NH = 16
DH = 64
KA_ROWS = 70


def fox_phase(cx, xs, w_in, w_out):
    nc, P, NT, SEQ = cx.nc, cx.P, cx.NT, cx.SEQ
    QA = nc.dram_tensor("fox_QA", [NH, KA_ROWS, NT], BF16).ap()
    KA = nc.dram_tensor("fox_KA", [NH, KA_ROWS, NT], BF16).ap()
    Vd = nc.dram_tensor("fox_V", [NT, D], BF16).ap()
    OTd = nc.dram_tensor("fox_OT", [D, NT], BF16).ap()
    xv = xview(xs)
    tps = SEQ // TT
    with ExitStack() as es:
        def sb(name, shape, dt):
            return es.enter_context(nc.sbuf_tensor(_u(name), shape, dt))

        ps, psR = mkps(cx, es, 6)
        Win, WinR = load_w(cx, es, "fWin", w_in, D, 3 * D + NH)
        xt2 = [sb("xt%d" % i, [128, NC, TT], F32) for i in range(2)]
        xt2R = [Reg(), Reg()]
        nb = NormBufs(cx, es)
        hT2 = [sb("hT%d" % i, [128, NC, TT], BF16) for i in range(2)]
        hT2R = [Reg(), Reg()]
        qk = [sb("qk%d" % i, [128, TT], BF16) for i in range(2)]
        qkR = [Reg(), Reg()]
        Vsb = sb("Vsb", [128, 4, D], BF16)
        VsbR = Reg()
        negb = sb("negb", [NH, 1], F32)
        negbR = Reg()
        e1 = sb("e1", [NH, TT], F32)
        e1R = Reg()
        onesf = sb("onesf", [NH, TT], F32)
        onesfR = Reg()
        cp = [sb("cp%d" % i, [NH, TT], F32) for i in range(2)]
        cpR = [Reg(), Reg()]
        r1 = sb("r1", [NH, TT], F32)
        r1R = Reg()
        spl = sb("spl", [NH, 6, TT], BF16)
        splR = Reg()
        ones3 = sb("ones3", [NH, 3, TT], BF16)
        ones3R = Reg()
        P.do("pool", lambda e: e.memset(onesf[:], 1.0), writes=[onesfR])
        P.do("pool", lambda e: e.memset(ones3[:], 1.0), writes=[ones3R])
        P.do("dve", lambda e: e.tensor_scalar(out=negb[:], in0=cx.pc("fox_b_f")[0:NH, :], scalar1=-1.0,
                                              scalar2=None, op0=ALU.mult), writes=[negbR])
        def pre_load(j):
            bb = j % 2
            P.ld("sp", "x%d" % bb, [(xt2[bb][:], xv[:, :, j * TT:(j + 1) * TT])], [xt2R[bb]])

        def pre(j):
            bb = j % 2
            rms_rstd(cx, nb, xt2[bb], xt2R[bb], ps[0], psR[0], skip_sq=True)
            rms_apply(cx, nb, xt2[bb], xt2R[bb], cx.pc("mix_norm_1"), hT2[bb], hT2R[bb])

        pre_load(0)
        rms_sq(cx, nb, xt2[0], xt2R[0])
        pre(0)
        for j in range(NT // TT):
            sl = slice(j * TT, (j + 1) * TT)
            first = (j % tps == 0)
            hT, hTR = hT2[j % 2], hT2R[j % 2]
            if j + 1 < NT // TT:
                pre_load(j + 1)
            for oc in range(16):
                if oc == 2 and j + 1 < NT // TT:
                    rms_sq(cx, nb, xt2[(j + 1) % 2], xt2R[(j + 1) % 2])
                if oc == 8 and j + 1 < NT // TT:
                    pre(j + 1)
                pb = 1 + oc % 2
                b = oc % 2
                P.mm(ps[pb][:], psR[pb], [(Win[:, c, oc * 128:(oc + 1) * 128], hT[:, c, :]) for c in range(NC)],
                     reads=[WinR, hTR])
                P.do("act", lambda e, pb=pb, b=b: e.activation(out=qk[b][:], in_=ps[pb][:], func=AF.Copy),
                     reads=[psR[pb]], writes=[qkR[b]])
                dst = QA if oc < 8 else KA
                h0 = 2 * (oc % 8)
                P.ld("sp", "qk%d" % b, [(dst[h0, 0:DH, sl], qk[b][0:DH, :]), (dst[h0 + 1, 0:DH, sl], qk[b][DH:128, :])],
                     [], reads=[qkR[b]])
            for tb in range(4):
                for nh in range(2):
                    pb = 3 + nh
                    P.mm(ps[pb][:], psR[pb],
                         [(hT[:, c, tb * 128:(tb + 1) * 128], Win[:, c, 2 * D + nh * 512:2 * D + (nh + 1) * 512])
                          for c in range(NC)], reads=[WinR, hTR])
                    P.do("dve", lambda e, pb=pb, tb=tb, nh=nh: e.tensor_copy(
                        out=Vsb[:, tb, nh * 512:(nh + 1) * 512], in_=ps[pb][:]),
                        reads=[psR[pb]], writes=[VsbR], nowaw=True)
            P.ld("sp", "vst", [(Vd[sl, :].rearrange("(tb p) f -> p tb f", p=128), Vsb[:])], [], reads=[VsbR])
            P.mm(ps[5][0:NH, :], psR[5], [(Win[:, c, 3 * D:3 * D + NH], hT[:, c, :]) for c in range(NC)],
                 reads=[WinR, hTR])
            P.do("act", lambda e: e.activation(out=e1[:], in_=ps[5][0:NH, :], func=AF.Exp, scale=-1.0,
                                               bias=negb[:]), reads=[psR[5], negbR], writes=[e1R])
            P.do("act", lambda e: e.activation(out=e1[:], in_=e1[:], func=AF.Ln, scale=1.0, bias=1.0),
                 reads=[e1R], writes=[e1R])
            b = j % 2
            if first:
                P.do("dve", lambda e, b=b: e.tensor_tensor_scan(out=cp[b][:], data0=onesf[:], data1=e1[:],
                                                                initial=0.0, op0=ALU.mult, op1=ALU.add),
                     reads=[onesfR, e1R], writes=[cpR[b]])
            else:
                P.do("dve", lambda e, b=b: e.tensor_tensor_scan(out=cp[b][:], data0=onesf[:], data1=e1[:],
                                                                initial=cp[1 - b][:, TT - 1:TT], op0=ALU.mult,
                                                                op1=ALU.add),
                     reads=[onesfR, e1R, cpR[1 - b]], writes=[cpR[b]])
            P.do("dve", lambda e, b=b: e.tensor_scalar(out=spl[:, 0, :], in0=cp[b][:], scalar1=8.0, scalar2=None,
                                                       op0=ALU.mult), reads=[cpR[b]], writes=[splR])
            P.do("dve", lambda e, b=b: e.scalar_tensor_tensor(out=r1[:], in0=cp[b][:], scalar=8.0, in1=spl[:, 0, :],
                                                              op0=ALU.mult, op1=ALU.subtract),
                 reads=[cpR[b], splR], writes=[r1R])
            P.do("dve", lambda e: e.tensor_copy(out=spl[:, 1, :], in_=r1[:]), reads=[r1R, splR], writes=[splR])
            P.do("dve", lambda e: e.tensor_tensor(out=r1[:], in0=r1[:], in1=spl[:, 1, :], op=ALU.subtract),
                 reads=[r1R, splR], writes=[r1R])
            P.do("dve", lambda e: e.tensor_copy(out=spl[:, 2, :], in_=r1[:]), reads=[r1R, splR], writes=[splR])
            P.do("dve", lambda e: e.tensor_scalar(out=spl[:, 3:6, :], in0=spl[:, 0:3, :], scalar1=-1.0, scalar2=None,
                                                  op0=ALU.mult), reads=[splR], writes=[splR])
            P.ld("sp", "spl", [(KA[:, DH:DH + 3, sl], spl[:, 0:3, :]), (QA[:, DH + 3:DH + 6, sl], spl[:, 3:6, :]),
                               (QA[:, DH:DH + 3, sl], ones3[:]), (KA[:, DH + 3:DH + 6, sl], ones3[:])],
                 [], reads=[splR, ones3R])
        P.barrier()
    nkt = SEQ // 128
    nqt = SEQ // TT
    with ExitStack() as es:
        def sb(name, shape, dt):
            return es.enter_context(nc.sbuf_tensor(_u(name), shape, dt))

        Qa = [sb("Qa%d" % i, [128, SEQ], BF16) for i in range(2)]
        Ka = [sb("Ka%d" % i, [128, SEQ], BF16) for i in range(2)]
        Va = [sb("Va%d" % i, [128, nkt, 128], BF16) for i in range(2)]
        QaR, KaR, VaR = [Reg(), Reg()], [Reg(), Reg()], [Reg(), Reg()]
        NPT = 3
        SK = 2
        psS = [es.enter_context(nc.psum_tensor(_u("psS"), [128, 2, TT], F32)) for _ in range(NPT)]
        psSR = [Reg() for _ in range(NPT)]
        psO = [es.enter_context(nc.psum_tensor(_u("psO"), [128, TT], F32)) for _ in range(2)]
        psOR = [Reg(), Reg()]
        Pt = [sb("Pt%d" % i, [128, 2, TT], BF16) for i in range(NPT)]
        PtR = [Reg() for _ in range(NPT)]
        rden = sb("rden", [128, TT], F32)
        rdenR = Reg()
        clp = sb("clp", [128, 2, TT], F32)
        clpR = Reg()
        rsh = sb("rsh", [DH, TT], F32)
        rshR = Reg()
        osb = [sb("osb%d" % i, [DH, TT], BF16) for i in range(2)]
        osbR = [Reg(), Reg()]
        for i in range(2):
            P.do("pool", lambda e, i=i: e.memset(Va[i][:, :, DH:128], 1.0), writes=[VaR[i]])
            P.do("pool", lambda e, i=i: e.memset(Qa[i][64:128, :], 0.0), writes=[QaR[i]])
            P.do("pool", lambda e, i=i: e.memset(Ka[i][64:128, :], 0.0), writes=[KaR[i]])
        nseq = NT // SEQ
        pairs = [(s, h) for s in range(nseq) for h in range(NH)]

        def load(ip):
            s, h = pairs[ip]
            b = ip % 2
            tsl = slice(s * SEQ, (s + 1) * SEQ)
            P.ld("sp", "qa%d" % b, [(Qa[b][0:KA_ROWS, :], QA[h, :, tsl])], [QaR[b]])
            P.ld("sp", "ka%d" % b, [(Ka[b][0:KA_ROWS, :], KA[h, :, tsl])], [KaR[b]])
            P.ld("sp", "va%d" % b,
                 [(Va[b][:, :, 0:DH], Vd[tsl, h * DH:(h + 1) * DH].rearrange("(kt p) f -> p kt f", p=128))],
                 [VaR[b]])

        nmask = [sb("nmask%d" % m, [128, 2, TT], BF16) for m in range(2)]
        nmaskR = Reg()
        for m in range(2):
            P.do("pool", lambda e, m=m: e.memset(nmask[m][:], 0.0), writes=[nmaskR])
            P.do("pool", lambda e, m=m: e.affine_select(
                out=nmask[m][:], in_=nmask[m][:], pattern=[[-128, 2], [1, TT]], compare_op=ALU.is_ge, fill=-30000.0,
                base=-256 * m, channel_multiplier=-1), reads=[nmaskR], writes=[nmaskR])
        idf = sb("idf", [128, 128], F32)
        identb = sb("identb", [128, 128], BF16)
        P.do("pool", lambda e: e.memset(idf[:], 1.0), writes=[nmaskR])
        P.do("pool", lambda e: e.affine_select(out=idf[:], in_=idf[:], pattern=[[1, 128]], compare_op=ALU.is_equal,
                                               fill=0.0, base=0, channel_multiplier=-1),
             reads=[nmaskR], writes=[nmaskR])
        P.do("pool", lambda e: e.tensor_copy(out=identb[:], in_=idf[:]), reads=[nmaskR], writes=[nmaskR])
        steps = []
        for ip in range(len(pairs)):
            for qi in range(nqt):
                for kp in range(2 * (qi + 1)):
                    steps.append((ip, qi, kp))

        def s_stage(t):
            ip, qi, kp = steps[t]
            b = ip % 2
            r = t % NPT
            qs = slice(qi * TT, (qi + 1) * TT)
            diag = kp >= 2 * qi
            m = kp - 2 * qi
            for a in range(2):
                ki = 2 * kp + a
                items = [(Ka[b][:, ki * 128:(ki + 1) * 128], Qa[b][:, qs])]
                if diag:
                    items.append((identb[:], nmask[m][:, a, :]))
                P.mm(psS[r][:, a, :], psSR[r], items, reads=[KaR[b], QaR[b], nmaskR])
            P.do("act", lambda e, r=r: e.activation(out=Pt[r][:], in_=psS[r][:], func=AF.Exp, scale=0.125),
                 reads=[psSR[r]], writes=[PtR[r]])

        qcount = [0]

        def pv_stage(t):
            ip, qi, kp = steps[t]
            s_, h = pairs[ip]
            b = ip % 2
            r = t % NPT
            npair = 2 * (qi + 1)
            ob = (ip * nqt + qi) % 2
            for a in range(2):
                ki = 2 * kp + a
                fst = (ki == 0)
                lst = (ki == 2 * npair - 1)
                P.op("pe", lambda e, r=r, ki=ki, a=a, ob=ob, b=b, fst=fst, lst=lst: e.matmul(
                    psO[ob][:], lhsT=Va[b][:, ki, :], rhs=Pt[r][:, a, :], start=fst, stop=lst),
                    P._deps("pe", [VaR[b], PtR[r]], [psOR[ob]] if fst else [], ()), sig=True)
                tok = P.last["pe"]
                _merge(PtR[r].r, tok)
                _merge(VaR[b].r, tok)
                if lst:
                    psOR[ob].w = {"pe": tok}
                    psOR[ob].r = {}
            if kp == npair - 1:
                P.do("dve", lambda e, ob=ob: e.reciprocal(out=rden[DH:128, :], in_=psO[ob][DH:128, :]),
                     reads=[psOR[ob]], writes=[rdenR])
                P.do("dve", lambda e: e.tensor_copy(out=rsh[:], in_=rden[DH:128, :]), reads=[rdenR], writes=[rshR])
                P.do("dve", lambda e, ob=ob: e.tensor_tensor(out=osb[ob][:], in0=psO[ob][0:DH, :], in1=rsh[:],
                                                             op=ALU.mult),
                     reads=[psOR[ob], rshR], writes=[osbR[ob]])
                P.ld("sp", "ot%d" % ob, [(OTd[h * DH:(h + 1) * DH, s_ * SEQ + qi * TT:s_ * SEQ + (qi + 1) * TT],
                                          osb[ob][:])], [], reads=[osbR[ob]])

        load(0)
        if len(pairs) > 1:
            load(1)

        def pv_and_prefetch(t):
            pv_stage(t)
            ip, qi, kp = steps[t]
            if qi == nqt - 1 and kp == 2 * nqt - 1 and ip + 2 < len(pairs):
                load(ip + 2)

        for t in range(len(steps)):
            s_stage(t)
            if t >= SK:
                pv_and_prefetch(t - SK)
        for t in range(max(0, len(steps) - SK), len(steps)):
            pv_and_prefetch(t)
        P.barrier()
    with ExitStack() as es:
        def sb(name, shape, dt):
            return es.enter_context(nc.sbuf_tensor(_u(name), shape, dt))

        psc, pscR = mkps(cx, es, 2)
        Wout, WoutR = load_w(cx, es, "fWout", w_out, D, D)
        xt = [sb("xt%d" % i, [128, NC, TT], F32) for i in range(2)]
        xtR = [Reg(), Reg()]
        ot = [sb("ot%d" % i, [128, NC, TT], BF16) for i in range(2)]
        otR = [Reg(), Reg()]
        otv = xview(OTd)
        for j in range(NT // TT):
            b = j % 2
            sl = slice(j * TT, (j + 1) * TT)
            P.ld("sp", "x%d" % b, [(xt[b][:], xv[:, :, sl])], [xtR[b]])
            P.ld("sp", "otl%d" % b, [(ot[b][:], otv[:, :, sl])], [otR[b]])
            for oc in range(NC):
                pb = oc % 2
                P.mm(psc[pb][:], pscR[pb], [(Wout[:, c, oc * 128:(oc + 1) * 128], ot[b][:, c, :]) for c in range(NC)],
                     reads=[WoutR, otR[b]])
                P.do("dve", lambda e, oc=oc, pb=pb, b=b: e.tensor_tensor(
                    out=xt[b][:, oc, :], in0=psc[pb][:], in1=xt[b][:, oc, :], op=ALU.add),
                    reads=[pscR[pb], xtR[b]], writes=[xtR[b]], nowaw=True)
            P.ld("sp", "o%d" % b, [(xv[:, :, sl], xt[b][:])], [], reads=[xtR[b]])
        P.barrier()


HH = 8
CH = 64


def hgrn_phase(cx, xs, w_in, w_out, layer=2):
    nc, P, NT, SEQ = cx.nc, cx.P, cx.NT, cx.SEQ
    xv = xview(xs)
    tps = SEQ // TT
    NCK = TT // CH
    with ExitStack() as es:
        ps, psR = mkps(cx, es, 7)
        def sb(name, shape, dt):
            return es.enter_context(nc.sbuf_tensor(_u(name), shape, dt))

        Win, WinR = load_w(cx, es, "hWin", w_in, D, 4 * D)
        Wout, WoutR = load_w(cx, es, "hWout", w_out, D, D)
        psT = es.enter_context(nc.psum_tensor(_u("psT"), [128, 2, TT], BF16))
        psTR = [Reg()]
        xt = sb("xt", [128, NC, TT], F32)
        xtR = Reg()
        nb = NormBufs(cx, es)
        hT = sb("hT", [128, NC, TT], BF16)
        hTR = Reg()
        Vsb = sb("Vsb", [128, 4, D], BF16)
        VsbR = Reg()
        F1 = sb("F1", [128, HH, TT], F32)
        F1R = Reg()
        G = sb("G", [128, HH, TT], F32)
        GR = Reg()
        QgT = sb("QgT", [128, HH, TT], BF16)
        QgTR = Reg()
        KdT = sb("KdT", [128, HH, TT], BF16)
        KdTR = Reg()
        Kd = sb("Kd", [128, HH, 4, 128], BF16)
        KdR = Reg()
        eGl = sb("eGl", [128, HH, NCK], F32)
        eGlR = Reg()
        eGp = sb("eGp", [128, HH], F32)
        eGpR = Reg()
        qs0 = sb("qs0", [128, TT], F32)
        qs = [qs0, qs0]
        qs0R = Reg()
        qsR = [qs0R, qs0R]
        cmask = sb("cmask", [128, TT], F32)
        cmaskR = Reg()
        m64 = sb("m64", [128, CH], F32)
        m64R = Reg()
        identb = sb("identb", [128, 128], BF16)
        identR = Reg()
        At = sb("At", [128, HH, CH], BF16)
        AtR = Reg()
        m64x = sb("m64x", [128, HH, CH], F32)
        U = sb("U", [128, HH, 128], F32)
        UR = [Reg() for _ in range(HH)]
        Sbf = sb("Sbf", [128, HH, 128], BF16)
        SbfR = [Reg() for _ in range(HH)]
        zT = sb("zT", [128, HH, TT], BF16)
        zTR = Reg()
        rs4 = [sb("rs%d" % i, [128, TT], F32) for i in range(2)]
        rs4R = [Reg() for _ in range(2)]
        lbt = sb("lbt", [128, 40], F32)
        lbR = Reg()
        lb = sb("lb", [128, HH], F32)
        oml = sb("oml", [128, HH], F32)
        gs = G[:].bitcast(BF16) if hasattr(G[:], "bitcast") else None
        P.do("pool", lambda e: e.memset(cmask[:], 1.0), writes=[cmaskR])
        cm3 = cmask[:].rearrange("p (c t) -> p c t", t=CH)
        P.do("pool", lambda e: e.memset(cm3[:, :, 0:1], 0.0), reads=[cmaskR], writes=[cmaskR])
        P.do("pool", lambda e: e.memset(m64x[:], 1.0), writes=[m64R])
        for hb in range(2):
            P.do("pool", lambda e, hb=hb: e.affine_select(
                out=m64x[hb * 64:(hb + 1) * 64, :, :], in_=m64x[hb * 64:(hb + 1) * 64, :, :],
                pattern=[[0, HH], [1, CH]], compare_op=ALU.is_ge, fill=0.0, base=0, channel_multiplier=-1),
                reads=[m64R], writes=[m64R])
        idf = sb("idf", [128, 128], F32)
        P.do("pool", lambda e: e.memset(idf[:], 1.0), writes=[identR])
        P.do("pool", lambda e: e.affine_select(out=idf[:], in_=idf[:], pattern=[[1, 128]], compare_op=ALU.is_equal,
                                               fill=0.0, base=0, channel_multiplier=-1),
             reads=[identR], writes=[identR])
        P.do("pool", lambda e: e.tensor_copy(out=identb[:], in_=idf[:]), reads=[identR], writes=[identR])
        P.do("act", lambda e: e.activation(out=lbt[:, 0:32], in_=cx.pc("hgrn_lb"), func=AF.Exp), writes=[lbR])
        P.do("dve", lambda e: e.tensor_tensor(out=lbt[:, 32:40], in0=lbt[:, 0:8], in1=lbt[:, 8:16], op=ALU.add),
             reads=[lbR], writes=[lbR])
        P.do("dve", lambda e: e.tensor_tensor(out=lbt[:, 32:40], in0=lbt[:, 32:40], in1=lbt[:, 16:24], op=ALU.add),
             reads=[lbR], writes=[lbR])
        P.do("dve", lambda e: e.tensor_tensor(out=lbt[:, 32:40], in0=lbt[:, 32:40], in1=lbt[:, 24:32], op=ALU.add),
             reads=[lbR], writes=[lbR])
        P.do("dve", lambda e: e.reciprocal(out=lbt[:, 32:40], in_=lbt[:, 32:40]), reads=[lbR], writes=[lbR])
        P.do("dve", lambda e: e.tensor_copy(out=lb[:], in_=lbt[:, 8:16]), reads=[lbR], writes=[lbR])
        for l in range(2, layer + 1):
            P.do("dve", lambda e, l=l: e.tensor_tensor(out=lb[:], in0=lb[:], in1=lbt[:, 8 * l:8 * l + 8], op=ALU.add),
                 reads=[lbR], writes=[lbR])
        P.do("dve", lambda e: e.tensor_tensor(out=lb[:], in0=lb[:], in1=lbt[:, 32:40], op=ALU.mult),
             reads=[lbR], writes=[lbR])
        P.do("dve", lambda e: e.tensor_scalar(out=oml[:], in0=lb[:], scalar1=-1.0, scalar2=1.0, op0=ALU.mult,
                                              op1=ALU.add), reads=[lbR], writes=[lbR])
        osb = F1
        psA, psAR = ps[3], Reg()
        psSt = [ps[4], ps[5]]
        psStR = [Reg(), Reg()]
        psO, psOR = ps[6], Reg()
        Gv = G[:].rearrange("p h (c t) -> p h c t", t=CH)

        for j in range(NT // TT):
            sl = slice(j * TT, (j + 1) * TT)
            first = (j % tps == 0)
            P.ld("sp", "x0", [(xt[:], xv[:, :, sl])], [xtR])
            rms_rstd(cx, nb, xt, xtR, ps[0], psR[0])
            rms_apply(cx, nb, xt, xtR, cx.pc("mix_norm_%d" % layer), hT, hTR)
            for tb in range(4):
                for nh in range(2):
                    pb = 1 + nh
                    P.mm(ps[pb][:], psR[pb],
                         [(hT[:, c, tb * 128:(tb + 1) * 128], Win[:, c, 2 * D + nh * 512:2 * D + (nh + 1) * 512])
                          for c in range(NC)], reads=[WinR, hTR])
                    P.do("dve", lambda e, pb=pb, tb=tb, nh=nh: e.tensor_copy(
                        out=Vsb[:, tb, nh * 512:(nh + 1) * 512], in_=ps[pb][:]),
                        reads=[psR[pb]], writes=[VsbR], nowaw=True)
            for h in range(HH):
                pb = 1 + h % 2
                P.mm(ps[pb][:], psR[pb], [(Win[:, c, D + h * 128:D + (h + 1) * 128], hT[:, c, :]) for c in range(NC)],
                     reads=[WinR, hTR])
                P.do("act", lambda e, pb=pb, h=h: e.activation(out=F1[:, h, :], in_=ps[pb][:], func=AF.Sigmoid),
                     reads=[psR[pb]], writes=[F1R], nowaw=True)
            for h in range(HH):
                pb = 1 + h % 2
                P.mm(ps[pb][:], psR[pb], [(Win[:, c, h * 128:(h + 1) * 128], hT[:, c, :]) for c in range(NC)],
                     reads=[WinR, hTR])
                P.do("act", lambda e, pb=pb, h=h: e.activation(out=QgT[:, h, :], in_=ps[pb][:], func=AF.Silu),
                     reads=[psR[pb]], writes=[QgTR], nowaw=(h > 0))
            for h in range(HH):
                P.do("dve", lambda e, h=h: e.tensor_scalar(out=F1[:, h, :], in0=F1[:, h, :], scalar1=oml[:, h:h + 1],
                                                           scalar2=lb[:, h:h + 1], op0=ALU.mult, op1=ALU.add),
                     reads=[F1R, lbR], writes=[F1R], nowaw=(h > 0))
            P.do("act", lambda e: e.activation(out=G[:], in_=F1[:], func=AF.Ln), reads=[F1R], writes=[GR])
            P.do("dve", lambda e: e.tensor_scalar(out=F1[:], in0=F1[:], scalar1=-1.0, scalar2=1.0, op0=ALU.mult,
                                                  op1=ALU.add), reads=[F1R], writes=[F1R])
            for h in range(HH):
                P.do("dve", lambda e, h=h: e.tensor_tensor_scan(out=G[:, h, :], data0=cmask[:], data1=G[:, h, :],
                                                                initial=0.0, op0=ALU.mult, op1=ALU.add),
                     reads=[GR, cmaskR], writes=[GR], nowaw=(h > 0))
            if not first:
                P.do("act", lambda e: e.activation(out=eGp[:], in_=eGl[:, :, NCK - 1], func=AF.Copy),
                     reads=[eGlR], writes=[eGpR])
            else:
                P.do("pool", lambda e: e.memset(eGp[:], 1.0), writes=[eGpR])
                P.do("pool", lambda e: e.memset(U[:], 0.0), writes=UR)
                P.do("pool", lambda e: e.memset(Sbf[:], 0.0), writes=SbfR)
            P.do("act", lambda e: e.activation(out=zT[:], in_=G[:], func=AF.Exp), reads=[GR], writes=[zTR])
            P.do("act", lambda e: e.activation(out=KdT[:], in_=G[:], func=AF.Exp, scale=-1.0),
                 reads=[GR], writes=[KdTR])
            P.do("act", lambda e: e.activation(out=eGl[:], in_=Gv[:, :, :, CH - 1], func=AF.Exp),
                 reads=[GR], writes=[eGlR])
            P.do("dve", lambda e: e.tensor_tensor(out=QgT[:], in0=QgT[:], in1=zT[:], op=ALU.mult),
                 reads=[QgTR, zTR], writes=[QgTR])
            P.do("dve", lambda e: e.tensor_tensor(out=KdT[:], in0=F1[:], in1=KdT[:], op=ALU.mult),
                 reads=[F1R, KdTR], writes=[KdTR])
            gsv = G[:].bitcast(BF16)[:, 0:HH * TT].rearrange("p (h t) -> p h t", t=TT) if False else None
            for h in range(HH):
                pb = 1 + h % 2
                P.mm(ps[pb][:], psR[pb],
                     [(Win[:, c, 3 * D + h * 128:3 * D + (h + 1) * 128], hT[:, c, :]) for c in range(NC)],
                     reads=[WinR, hTR])
                P.do("act", lambda e, pb=pb, h=h: e.activation(out=G[:, h, :], in_=ps[pb][:], func=AF.Silu),
                     reads=[psR[pb]], writes=[GR], nowaw=(h > 0))
            for hp in range(HH // 2):
                deps = P._deps("pe", [KdTR, identR], [psTR[0]], ())
                tok = None
                n = 0
                for hb in range(2):
                    h = 2 * hp + hb
                    for blk in range(4):
                        tok = P.op("pe", lambda e, h=h, hb=hb, blk=blk: e.transpose(
                            psT[:, hb, blk * 128:(blk + 1) * 128], KdT[:, h, blk * 128:(blk + 1) * 128], identb[:]),
                            deps if n == 0 else (), sig=(n == 7))
                        n += 1
                P._mark(tok, [KdTR, identR], [psTR[0]])
                P.do("act", lambda e, hp=hp: e.activation(
                    out=Kd[:, 2 * hp:2 * hp + 2, :, :],
                    in_=psT[:].rearrange("p a (b k) -> p a b k", k=128), func=AF.Copy),
                    reads=[psTR[0]], writes=[KdR], nowaw=(hp > 0))
            for ch in range(NCK):
                base = (ch % 2) * 64
                blk = ch // 2
                cols = slice(ch * CH, (ch + 1) * CH)
                rows = slice(base, base + 64)
                for h in range(HH):
                    P.mm(psA[rows, h * CH:(h + 1) * CH], psAR, [(KdT[:, h, cols], QgT[:, h, cols])],
                         reads=[KdTR, QgTR])
                P.do("dve", lambda e, rows=rows: e.tensor_tensor(
                    out=At[rows, :, :], in0=psA[rows, :].rearrange("p (h t) -> p h t", t=CH), in1=m64x[rows, :, :],
                    op=ALU.mult), reads=[psAR, m64R], writes=[AtR])
                for h in range(HH):
                    vcols = slice(h * 128, (h + 1) * 128)
                    P.mm(psSt[h // 4][:, (h % 4) * 128:(h % 4 + 1) * 128], psStR[h // 4],
                         [(Kd[rows, h, blk, :], Vsb[rows, blk, vcols])], reads=[KdR, VsbR])
                for h in range(HH):
                    vcols = slice(h * 128, (h + 1) * 128)
                    P.mm(psO[:, h * CH:(h + 1) * CH], psOR,
                         [(Sbf[:, h, :], QgT[:, h, cols]), (Vsb[rows, blk, vcols], At[rows, h, :])],
                         reads=[SbfR[h], QgTR, VsbR, AtR])
                for h in range(HH):
                    sc = eGp[:, h:h + 1] if ch == 0 else eGl[:, h, ch - 1:ch]
                    P.do("dve", lambda e, h=h, sc=sc: e.scalar_tensor_tensor(
                        out=U[:, h, :], in0=U[:, h, :], scalar=sc, in1=psSt[h // 4][:, (h % 4) * 128:(h % 4 + 1) * 128],
                        op0=ALU.mult, op1=ALU.add), reads=[UR[h], psStR[h // 4], eGlR, eGpR], writes=[UR[h]])
                    P.do("act", lambda e, h=h, ch=ch: e.activation(out=Sbf[:, h, :], in_=U[:, h, :], func=AF.Copy,
                                                                   scale=eGl[:, h, ch:ch + 1]),
                         reads=[UR[h], eGlR], writes=[SbfR[h]])
                P.do("act", lambda e, cols=cols: e.activation(
                    out=osb[:, :, cols], in_=psO[:].rearrange("p (h t) -> p h t", t=CH), func=AF.Copy),
                    reads=[psOR, KdTR], writes=[F1R], nowaw=(ch > 0))
            P.do("act", lambda e: e.activation(out=nb.sq[:], in_=osb[:], func=AF.Square), reads=[F1R],
                 writes=[nb.sqR])
            pbanks = [0, 3, 4, 5]

            def post_mm(h):
                pb = pbanks[h % 4]
                P.mm(ps[pb][:], psR[pb], [(nb.ones[:], nb.sq[:, h, :])], reads=[nb.sqR, nb.onesR])

            for h in range(4):
                post_mm(h)
            for h in range(HH):
                pb = pbanks[h % 4]
                rb = h % 2
                P.do("act", lambda e, pb=pb, rb=rb: e.activation(out=rs4[rb][:], in_=ps[pb][:], func=AF.Ln,
                                                                 bias=nb.eps[:], scale=1.0 / 128),
                     reads=[psR[pb], nb.onesR], writes=[rs4R[rb]])
                P.do("act", lambda e, rb=rb: e.activation(out=rs4[rb][:], in_=rs4[rb][:], func=AF.Exp, scale=-0.5),
                     reads=[rs4R[rb]], writes=[rs4R[rb]])
                if h + 4 < HH:
                    post_mm(h + 4)
                P.do("dve", lambda e, h=h, rb=rb: e.tensor_tensor(out=osb[:, h, :], in0=osb[:, h, :], in1=rs4[rb][:],
                                                                  op=ALU.mult),
                     reads=[F1R, rs4R[rb]], writes=[F1R], nowaw=(h > 0))
                P.do("dve", lambda e, h=h: e.scalar_tensor_tensor(
                    out=zT[:, h, :], in0=osb[:, h, :], scalar=cx.pc("hgrn_norm", h, 1), in1=G[:, h, :],
                    op0=ALU.mult, op1=ALU.mult), reads=[F1R, GR], writes=[zTR], nowaw=(h > 0))
            for oc in range(NC):
                pb = 1 + oc % 2
                P.mm(ps[pb][:], psR[pb], [(Wout[:, h, oc * 128:(oc + 1) * 128], zT[:, h, :]) for h in range(HH)],
                     reads=[WoutR, zTR])
                P.do("dve", lambda e, oc=oc, pb=pb: e.tensor_tensor(
                    out=xt[:, oc, :], in0=ps[pb][:], in1=xt[:, oc, :], op=ALU.add),
                    reads=[psR[pb], xtR], writes=[xtR], nowaw=True)
            P.ld("sp", "o0", [(xv[:, :, sl], xt[:])], [], reads=[xtR])
        P.barrier()


def build_test(NT, SEQ, phase, dram_specs, ins_order=None):
    nc = bass.Bass("TRN2", target_bir_lowering=False)
    T = {}
    T["xT"] = nc.dram_tensor("xT", [D, NT], F32, kind="ExternalInput").ap()
    T["par"] = nc.dram_tensor("par", [128, NPAR], F32, kind="ExternalInput").ap()
    for k, shp in dram_specs.items():
        T[k] = nc.dram_tensor(k, list(shp), F32, kind="ExternalInput").ap()
    T["yT"] = nc.dram_tensor("yT", [D, NT], F32, kind="ExternalOutput").ap()
    P = Prog(nc)
    cx = Ctx(nc, P, NT, SEQ)
    with ExitStack() as es:
        par_sb = es.enter_context(nc.sbuf_tensor("par_sb", [128, NPAR], F32))
        cx.par = par_sb
        P.dma("sp", "par", par_sb[:], T["par"][:, :])
        P.barrier()
        phase(cx, T)
        P.barrier()
        P.emit()
    return nc


W_SPECS = {
    "ffn_w_gate": (4, 2, D, DFF), "ffn_w_up": (4, 2, D, DFF), "ffn_w_down": (4, 2, DFF, D),
    "conv_w_in": (1, D, 2 * D), "conv_w_out": (1, D, D),
    "fox_w_in": (1, D, 3 * D + NH), "fox_w_out": (1, D, D),
    "hgrn_w_in": (1, D, 4 * D), "hgrn_w_out": (1, D, D),
    "pool_w": (1, 4, 256, 256),
}


def build_full(NT=8192, SEQ=4096, depth=4):
    nc = bass.Bass("TRN2", target_bir_lowering=False)
    T = {}
    T["xT"] = nc.dram_tensor("xT", [D, NT], F32, kind="ExternalInput").ap()
    T["par"] = nc.dram_tensor("par", [128, NPAR], F32, kind="ExternalInput").ap()
    for k, shp in W_SPECS.items():
        T[k] = nc.dram_tensor(k, list(shp), F32, kind="ExternalInput").ap()
    yT = nc.dram_tensor("yT", [D, NT], F32, kind="ExternalOutput").ap()
    P = Prog(nc)
    cx = Ctx(nc, P, NT, SEQ)
    with ExitStack() as es:
        par_sb = es.enter_context(nc.sbuf_tensor("par_sb", [128, NPAR], F32))
        cx.par = par_sb
        P.dma("sp", "par", par_sb[:], T["par"][:, :])
        P.barrier()
        for i in range(depth):
            src = T["xT"] if i == 0 else yT
            ffn_phase(cx, src, yT, T["ffn_w_gate"][i, 0], T["ffn_w_up"][i, 0], T["ffn_w_down"][i, 0],
                      cx.pc("ffn_norm_%d_0" % i))
            m = i % 4
            if m == 0:
                conv_phase(cx, yT, T["conv_w_in"][0], T["conv_w_out"][0])
            elif m == 1:
                fox_phase(cx, yT, T["fox_w_in"][0], T["fox_w_out"][0])
            elif m == 2:
                hgrn_phase(cx, yT, T["hgrn_w_in"][0], T["hgrn_w_out"][0], layer=i)
            else:
                pool_phase(cx, yT, T["pool_w"][0])
            ffn_phase(cx, yT, yT, T["ffn_w_gate"][i, 1], T["ffn_w_up"][i, 1], T["ffn_w_down"][i, 1],
                      cx.pc("ffn_norm_%d_1" % i), final_g=(cx.pc("final_norm") if i == depth - 1 else None))
        P.barrier()
        P.emit()
    return nc


_NC_CACHE = {}


def kernel(**inputs):
    x = np.asarray(inputs["x"], dtype=np.float32)
    B, S, Dm = x.shape
    bpc = B // NCORES
    NT = bpc * S
    key = (NT, S)
    if key not in _NC_CACHE:
        _NC_CACHE[key] = build_full(NT, S)
    nc = _NC_CACHE[key]
    par = pack_params(inputs)
    ws = {k: np.ascontiguousarray(np.asarray(inputs[k], dtype=np.float32)) for k in W_SPECS}
    in_maps = []
    for c in range(NCORES):
        xc = x[c * bpc:(c + 1) * bpc].reshape(NT, Dm)
        m = {"xT": np.ascontiguousarray(xc.T), "par": par}
        m.update(ws)
        in_maps.append(m)
    res = run_bass_kernel_spmd(nc, in_maps, core_ids=list(range(NCORES)))
    outs = [np.asarray(r["yT"]).T.reshape(bpc, S, Dm) for r in res.results]
    return np.ascontiguousarray(np.concatenate(outs, axis=0)).astype(np.float32)
```

```python
from contextlib import ExitStack
import numpy as np
import concourse.bass as bass
import concourse.mybir as mybir
from concourse.bass_utils import run_bass_kernel_spmd

F32 = mybir.dt.float32
BF16 = mybir.dt.bfloat16
AF = mybir.ActivationFunctionType
ALU = mybir.AluOpType

D = 1024
DFF = 2816
NFF = DFF // 128
NC = 8
TT = 512
RMS_EPS = 1e-6
LN_EPS = 1e-5
NCORES = 8

_UID = [0]


def _u(name):
    _UID[0] += 1
    return "%s_%d" % (name, _UID[0])


_BLK = dict(pe="tensor", act="scalar", dve="vector", pool="gpsimd", sp="sync")


class Reg:
    __slots__ = ("w", "r")

    def __init__(self):
        self.w = {}
        self.r = {}


def _merge(d, tok):
    if tok is None:
        return
    k = tok[0]
    if k not in d or d[k][2] < tok[2]:
        d[k] = tok


class Prog:
    ENG = ["pe", "act", "dve", "pool", "sp"]

    def __init__(self, nc):
        self.nc = nc
        self.q = {k: [] for k in self.ENG}
        self.sem = {k: nc.alloc_semaphore(name="sem_" + k) for k in self.ENG}
        self.cnt = {k: 0 for k in self.ENG}
        self.seen = {k: {} for k in self.ENG}
        self.dsem = {}
        self.last = {}

    def _waits(self, eng, deps):
        ws = []
        for d in deps:
            if d is None:
                continue
            key, sem, val = d
            if key == "pe" and eng == "pe":
                continue
            if self.seen[eng].get(key, 0) < val:
                self.seen[eng][key] = val
                ws.append((sem, val))
        return ws

    def op(self, eng, fn, deps=(), sig=True):
        ws = self._waits(eng, deps)
        tok = None
        inc = None
        if sig:
            self.cnt[eng] += 1
            tok = (eng, self.sem[eng], self.cnt[eng])
            inc = (self.sem[eng], 1)
            self.last[eng] = tok
        self.q[eng].append((ws, fn, inc))
        return tok

    def dma(self, eng, slot, out, in_, deps=(), **kw):
        if slot not in self.dsem:
            self.dsem[slot] = [self.nc.alloc_semaphore(name="d_" + slot), 0]
        s = self.dsem[slot]
        s[1] += 16
        ws = self._waits(eng, deps)
        self.q[eng].append((ws, lambda e: e.dma_start(out=out, in_=in_, **kw), (s[0], 16)))
        tok = ("d_" + slot, s[0], s[1])
        self.last["d_" + slot] = tok
        return tok

    @staticmethod
    def _deps(eng, reads, writes, extra, nowaw=False):
        deps = [t for t in extra if t is not None]
        for R in reads:
            deps += list(R.w.values())
        for R in writes:
            for t in R.w.values():
                if nowaw and t[0] == eng:
                    continue
                deps.append(t)
            deps += list(R.r.values())
        return deps

    @staticmethod
    def _mark(tok, reads, writes):
        for R in reads:
            _merge(R.r, tok)
        for R in writes:
            R.w = {tok[0]: tok}
            R.r = {}

    def do(self, eng, fn, reads=(), writes=(), extra=(), nowaw=False):
        tok = self.op(eng, fn, self._deps(eng, reads, writes, extra, nowaw))
        self._mark(tok, reads, writes)
        return tok

    def mm(self, out_ap, out_reg, items, reads, extra=(), tp=None):
        deps = self._deps("pe", reads, [out_reg], extra)
        n = len(items)
        tok = None
        for i, (l, r) in enumerate(items):
            tok = self.op("pe", lambda e, l=l, r=r, i=i: e.matmul(out_ap, lhsT=l, rhs=r, start=(i == 0),
                                                                   stop=(i == n - 1)),
                          deps if i == 0 else (), sig=(i == n - 1))
        self._mark(tok, reads, [out_reg])
        return tok

    def ld(self, eng, slot, pairs, writes, reads=(), extra=(), **kw):
        deps = self._deps("d_" + slot, reads, writes, extra)
        tok = None
        for i, (o, s) in enumerate(pairs):
            tok = self.dma(eng, slot, o, s, deps if i == 0 else (), **kw)
        self._mark(tok, reads, writes)
        return tok

    def barrier(self):
        toks = list(self.last.values())
        for k in self.ENG:
            ws = self._waits(k, toks)
            if ws:
                self.q[k].append((ws, None, None))

    def emit(self):
        with self.nc.Block() as block:
            for k in self.ENG:
                if not self.q[k]:
                    continue

                def body(e, k=k):
                    for ws, fn, inc in self.q[k]:
                        for sem, val in ws:
                            e.wait_ge(sem, val)
                        if fn is None:
                            continue
                        ins = fn(e)
                        if inc is not None:
                            ins.then_inc(inc[0], inc[1])

                getattr(block, _BLK[k])(body)


PAR = {}


def _par_layout():
    off = 0
    lay = {}

    def add(name, ncol):
        nonlocal off
        lay[name] = (off, ncol)
        off += ncol

    for i in range(4):
        for k in range(2):
            add("ffn_norm_%d_%d" % (i, k), 8)
        add("mix_norm_%d" % i, 8)
    add("final_norm", 8)
    add("conv_b_in", 16)
    add("conv_dw", 31 * 8)
    add("conv_dw_b", 8)
    add("conv_ln_g", 8)
    add("conv_ln_b", 8)
    add("hgrn_lb", 32)
    add("hgrn_norm", 8)
    add("pool_scale", 8)
    add("fox_b_f", 1)
    lay["_n"] = (off, 0)
    return lay


PAR = _par_layout()
NPAR = PAR["_n"][0]


def _pcol(v):
    v = np.asarray(v, dtype=np.float32).reshape(-1, 128)
    return np.ascontiguousarray(v.T)


def pack_params(inp):
    par = np.zeros((128, NPAR), np.float32)

    def put(name, arr):
        o, n = PAR[name]
        par[:, o:o + n] = arr

    for i in range(4):
        for k in range(2):
            put("ffn_norm_%d_%d" % (i, k), _pcol(inp["ffn_norm"][i, k]))
        put("mix_norm_%d" % i, _pcol(inp["mix_norm"][i]))
    put("final_norm", _pcol(inp["final_norm"]))
    put("conv_b_in", _pcol(inp["conv_b_in"][0]))
    dw = np.asarray(inp["conv_dw"][0], np.float32)
    put("conv_dw", np.ascontiguousarray(dw.reshape(31, 8, 128).transpose(2, 0, 1).reshape(128, 248)))
    put("conv_dw_b", _pcol(inp["conv_dw_b"][0]))
    put("conv_ln_g", _pcol(inp["conv_ln_g"][0]))
    put("conv_ln_b", _pcol(inp["conv_ln_b"][0]))
    lbl = np.asarray(inp["hgrn_lb_logits"], np.float32)
    put("hgrn_lb", np.ascontiguousarray(lbl.reshape(4, 8, 128).transpose(2, 0, 1).reshape(128, 32)))
    put("hgrn_norm", _pcol(inp["hgrn_norm"][0]))
    put("pool_scale", _pcol(inp["pool_scale"][0]))
    bf = np.zeros((128, 1), np.float32)
    bf[:16, 0] = np.asarray(inp["fox_b_f"][0], np.float32)
    put("fox_b_f", bf)
    return par


class Ctx:
    def __init__(self, nc, P, NT, SEQ):
        self.nc, self.P, self.NT, self.SEQ = nc, P, NT, SEQ
        self.par = None
        self.ps = None
        self.psR = None

    def pc(self, name, c0=0, n=None):
        o, m = PAR[name]
        if n is None:
            n = m - c0
        return self.par[:, o + c0:o + c0 + n]


def mkps(cx, es, n=7):
    ps = [es.enter_context(cx.nc.psum_tensor(_u("ps%d" % i), [128, TT], F32)) for i in range(n)]
    psR = [Reg() for _ in range(n)]
    cx.ps, cx.psR = ps, psR
    return ps, psR


def _wsplit(n):
    for d in (2048, 1544, 1408, 1024, 512, 256, 128):
        if n % d == 0 and d <= 2048:
            return d
    return n


def load_w(cx, es, name, w2d, K, N):
    nc, P = cx.nc, cx.P
    kc = K // 128
    W = es.enter_context(nc.sbuf_tensor(_u(name), [128, kc, N], BF16))
    R = Reg()
    v = w2d.rearrange("(c p) f -> p c f", p=128)
    P.ld("pool", name, [(W[:, c, :], v[:, c, :]) for c in range(kc)], [R], max_dma_last_dim=_wsplit(N) * 4)
    return W, R


class NormBufs:
    def __init__(self, cx, es):
        nc = cx.nc
        self.sq = es.enter_context(nc.sbuf_tensor(_u("nb_sq"), [128, NC, TT], BF16))
        self.rstd = es.enter_context(nc.sbuf_tensor(_u("nb_rstd"), [128, TT], F32))
        self.ones = es.enter_context(nc.sbuf_tensor(_u("nb_ones"), [128, 128], BF16))
        self.eps = es.enter_context(nc.sbuf_tensor(_u("nb_eps"), [128, 1], F32))
        self.sqR, self.rstdR, self.onesR = Reg(), Reg(), Reg()
        cx.P.do("pool", lambda e: e.memset(self.ones[:], 1.0), writes=[self.onesR])
        cx.P.do("pool", lambda e: e.memset(self.eps[:], RMS_EPS), writes=[self.onesR])


def rms_sq(cx, nb, xt, xtR):
    cx.P.do("act", lambda e: e.activation(out=nb.sq[:], in_=xt[:], func=AF.Square), reads=[xtR], writes=[nb.sqR])


def rms_rstd(cx, nb, xt, xtR, psS, psSR, skip_sq=False):
    P = cx.P
    if not skip_sq:
        rms_sq(cx, nb, xt, xtR)
    P.mm(psS[:], psSR, [(nb.ones[:], nb.sq[:, c, :]) for c in range(NC)], reads=[nb.sqR, nb.onesR])
    P.do("act", lambda e: e.activation(out=nb.rstd[:], in_=psS[:], func=AF.Ln, bias=nb.eps[:], scale=1.0 / D),
         reads=[psSR], writes=[nb.rstdR])
    P.do("act", lambda e: e.activation(out=nb.rstd[:], in_=nb.rstd[:], func=AF.Exp, scale=-0.5),
         reads=[nb.rstdR], writes=[nb.rstdR])


def rms_apply(cx, nb, xt, xtR, gcol, out3, outR, off=0):
    P = cx.P
    for c in range(NC):
        P.do("dve", lambda e, c=c: e.scalar_tensor_tensor(
            out=out3[:, c, off:off + TT], in0=xt[:, c, :], scalar=gcol[:, c:c + 1], in1=nb.rstd[:],
            op0=ALU.mult, op1=ALU.mult), reads=[xtR, nb.rstdR], writes=[outR], nowaw=True)


def xview(ap):
    return ap.rearrange("(c p) t -> p c t", p=128)


def ffn_phase(cx, src, dst, wg, wu, wd, gcol, final_g=None):
    nc, P, NT = cx.nc, cx.P, cx.NT
    ntile = NT // TT
    with ExitStack() as es:
        def sb(name, shape, dt):
            return es.enter_context(nc.sbuf_tensor(_u(name), shape, dt))

        mkps(cx, es)
        Wg = sb("Wg", [128, NC, DFF], BF16)
        Wu = sb("Wu", [128, NC, DFF], BF16)
        fblk = [(0, 6), (6, 12), (12, 17), (17, 22)]
        WgR = [Reg() for _ in fblk]
        WuR = [Reg() for _ in fblk]
        wgv = wg.rearrange("(c p) f -> p c f", p=128)
        wuv = wu.rearrange("(c p) f -> p c f", p=128)
        for bi, (f0, f1) in enumerate(fblk):
            cs_ = slice(f0 * 128, f1 * 128)
            P.ld("pool", "wg%d" % bi, [(Wg[:, :, cs_], wgv[:, :, cs_])], [WgR[bi]])
            P.ld("pool", "wu%d" % bi, [(Wu[:, :, cs_], wuv[:, :, cs_])], [WuR[bi]])
        blk_of = {}
        for bi, (f0, f1) in enumerate(fblk):
            for f in range(f0, f1):
                blk_of[f] = bi
        Wd, WdR = load_w(cx, es, "Wd", wd, DFF, D)
        xt = [sb("xt%d" % i, [128, NC, TT], F32) for i in range(2)]
        xtR = [Reg(), Reg()]
        nb = NormBufs(cx, es)
        hT = sb("hT", [128, NC, TT], BF16)
        hTR = Reg()
        hid = sb("hid", [128, NFF, TT], BF16)
        hidR = [Reg() for _ in range(NFF)]
        psS, psSR = cx.ps[0], cx.psR[0]
        psG, psGR = cx.ps[1:3], cx.psR[1:3]
        psU, psUR = cx.ps[3:5], cx.psR[3:5]
        psD, psDR = cx.ps[5:7], cx.psR[5:7]
        srcv, dstv = xview(src), xview(dst)

        def x_load(j):
            b = j % 2
            P.ld("sp", "x%d" % b, [(xt[b][:], srcv[:, :, j * TT:(j + 1) * TT])], [xtR[b]])

        def norm_b(j):
            b = j % 2
            rms_rstd(cx, nb, xt[b], xtR[b], psS, psSR, skip_sq=True)
            rms_apply(cx, nb, xt[b], xtR[b], gcol, hT, hTR)

        def fin_b(j):
            b = j % 2
            rms_rstd(cx, nb, xt[b], xtR[b], psS, psSR, skip_sq=True)
            rms_apply(cx, nb, xt[b], xtR[b], final_g, xt[b], xtR[b])
            P.ld("sp", "o%d" % b, [(dstv[:, :, j * TT:(j + 1) * TT], xt[b][:])], [], reads=[xtR[b]])

        def gu_stage(j, hooks):
            for f in range(NFF):
                for hk in hooks.get(f, ()):
                    hk()
                pb = f % 2
                fs = slice(f * 128, (f + 1) * 128)
                P.mm(psG[pb][:], psGR[pb], [(Wg[:, c, fs], hT[:, c, :]) for c in range(NC)],
                     reads=[hTR, WgR[blk_of[f]]])
                P.mm(psU[pb][:], psUR[pb], [(Wu[:, c, fs], hT[:, c, :]) for c in range(NC)],
                     reads=[hTR, WuR[blk_of[f]]])
                P.do("act", lambda e, f=f, pb=pb: e.activation(out=hid[:, f, :], in_=psG[pb][:], func=AF.Silu),
                     reads=[psGR[pb]], writes=[hidR[f]])
                P.do("dve", lambda e, f=f, pb=pb: e.tensor_tensor(out=hid[:, f, :], in0=hid[:, f, :],
                                                                  in1=psU[pb][:], op=ALU.mult),
                     reads=[psUR[pb], hidR[f]], writes=[hidR[f]])

        def down_stage(j):
            b = j % 2
            sl = slice(j * TT, (j + 1) * TT)
            for c in range(NC):
                pb = c % 2
                cs = slice(c * 128, (c + 1) * 128)
                P.mm(psD[pb][:], psDR[pb], [(Wd[:, f, cs], hid[:, f, :]) for f in range(NFF)],
                     reads=hidR + [WdR])
                P.do("dve", lambda e, c=c, pb=pb: e.scalar_tensor_tensor(
                    out=xt[b][:, c, :], in0=psD[pb][:], scalar=0.5, in1=xt[b][:, c, :],
                    op0=ALU.mult, op1=ALU.add), reads=[psDR[pb], xtR[b]], writes=[xtR[b]], nowaw=True)
            if final_g is None:
                P.ld("sp", "o%d" % b, [(dstv[:, :, sl], xt[b][:])], [], reads=[xtR[b]])

        x_load(0)
        rms_sq(cx, nb, xt[0], xtR[0])
        norm_b(0)
        for j in range(ntile):
            hooks = {}
            fin = final_g is not None and j >= 1
            if fin:
                hooks[2] = [lambda j=j: rms_sq(cx, nb, xt[(j - 1) % 2], xtR[(j - 1) % 2])]
                hooks[8] = [lambda j=j: fin_b(j - 1)]
            if j + 1 < ntile:
                hooks.setdefault(9 if fin else 0, []).append(lambda j=j: x_load(j + 1))
                hooks.setdefault(14, []).append(lambda j=j: rms_sq(cx, nb, xt[(j + 1) % 2], xtR[(j + 1) % 2]))
            gu_stage(j, hooks)
            if j + 1 < ntile:
                norm_b(j + 1)
            down_stage(j)
        if final_g is not None:
            rms_sq(cx, nb, xt[(ntile - 1) % 2], xtR[(ntile - 1) % 2])
            fin_b(ntile - 1)
        P.barrier()


def final_phase(cx, src, dst, gcol):
    nc, P, NT = cx.nc, cx.P, cx.NT
    with ExitStack() as es:
        xt = [es.enter_context(nc.sbuf_tensor(_u("xt%d" % i), [128, NC, TT], F32)) for i in range(2)]
        xtR = [Reg(), Reg()]
        nb = NormBufs(cx, es)
        mkps(cx, es, 1)
        srcv, dstv = xview(src), xview(dst)
        for j in range(NT // TT):
            b = j % 2
            sl = slice(j * TT, (j + 1) * TT)
            P.ld("sp", "x%d" % b, [(xt[b][:], srcv[:, :, sl])], [xtR[b]])
            rms_rstd(cx, nb, xt[b], xtR[b], cx.ps[0], cx.psR[0])
            rms_apply(cx, nb, xt[b], xtR[b], gcol, xt[b], xtR[b])
            P.ld("sp", "o%d" % b, [(dstv[:, :, sl], xt[b][:])], [], reads=[xtR[b]])
        P.barrier()


def copy_phase(cx, src, dst):
    nc, P, NT = cx.nc, cx.P, cx.NT
    with ExitStack() as es:
        xt = [es.enter_context(nc.sbuf_tensor(_u("xt%d" % i), [128, NC, TT], F32)) for i in range(2)]
        xtR = [Reg(), Reg()]
        srcv, dstv = xview(src), xview(dst)
        for j in range(NT // TT):
            b = j % 2
            sl = slice(j * TT, (j + 1) * TT)
            P.ld("sp", "x%d" % b, [(xt[b][:], srcv[:, :, sl])], [xtR[b]])
            P.ld("sp", "o%d" % b, [(dstv[:, :, sl], xt[b][:])], [], reads=[xtR[b]])
        P.barrier()


def pool_phase(cx, xs, pool_w):
    nc, P, NT, SEQ = cx.nc, cx.P, cx.NT, cx.SEQ
    H = 16
    with ExitStack() as es:
        def sb(name, shape, dt):
            return es.enter_context(nc.sbuf_tensor(_u(name), shape, dt))

        pps, ppsR = mkps(cx, es, 3)
        Wp = sb("Wp", [128, 8, 256], BF16)
        WpR = Reg()
        wv = pool_w.rearrange("g (k p) n -> p (g k) n", p=128)
        P.ld("pool", "Wp", [(Wp[:, i, :], wv[:, i, :]) for i in range(8)], [WpR])
        xt2 = [sb("xt%d" % i, [128, NC, TT], F32) for i in range(2)]
        xt2R = [Reg(), Reg()]
        nb = NormBufs(cx, es)
        hf = sb("hf", [128, NC, H + TT], F32)
        hfR = Reg()
        sA = sb("sA", [128, 2, H + TT], F32)
        sB = sb("sB", [128, 2, H + TT], F32)
        sAR, sBR = Reg(), Reg()
        mT = sb("mT", [128, NC, TT], BF16)
        mTR = Reg()
        rc = sb("rc", [128, 4, H], F32)
        rcR = Reg()
        tmp = sb("ptmp", [128, H], F32)
        tmpR = Reg()
        for gi in range(4):
            P.do("pool", lambda e, gi=gi: e.iota(rc[:, gi, :], [[1, H]], base=1, channel_multiplier=0,
                                                 allow_small_or_imprecise_dtypes=True), writes=[rcR])
        for gi in range(4):
            P.do("dve", lambda e, gi=gi: e.tensor_scalar(out=rc[:, gi, :], in0=rc[:, gi, :],
                                                         scalar1=float(2 ** (gi + 1)), scalar2=None, op0=ALU.min),
                 reads=[rcR], writes=[rcR])
        P.do("dve", lambda e: e.reciprocal(out=rc[:], in_=rc[:]), reads=[rcR], writes=[rcR])
        xv = xview(xs)
        psS, psSR = pps[0], ppsR[0]
        tps = SEQ // TT
        ntile = NT // TT

        def pre(j):
            bb = j % 2
            P.ld("sp", "x%d" % bb, [(xt2[bb][:], xv[:, :, j * TT:(j + 1) * TT])], [xt2R[bb]])
            rms_rstd(cx, nb, xt2[bb], xt2R[bb], psS, psSR)

        pre(0)
        for j in range(ntile):
            sl = slice(j * TT, (j + 1) * TT)
            first = (j % tps == 0)
            xt, xtR = xt2[j % 2], xt2R[j % 2]
            if first:
                P.do("pool", lambda e: e.memset(hf[:, :, 0:H], 0.0), writes=[hfR])
            else:
                P.do("act", lambda e: e.activation(out=hf[:, :, 0:H], in_=hf[:, :, TT:TT + H], func=AF.Copy),
                     reads=[hfR], writes=[hfR])
            rms_apply(cx, nb, xt, xtR, cx.pc("mix_norm_3"), hf, hfR, off=H)
            for gi in range(4):
                w = 2 ** (gi + 1)
                cur, curR = hf[:, 2 * gi:2 * gi + 2, :], hfR
                L = H + TT
                sh = 1
                lo = 0
                bufs = [(sA, sAR), (sB, sBR)]
                bi = 0
                while sh < w:
                    o, oR = bufs[bi]
                    lo2 = lo + sh
                    P.do("dve", lambda e, o=o, cur=cur, lo2=lo2, sh=sh, L=L: e.tensor_tensor(
                        out=o[:, :, lo2:L], in0=cur[:, :, lo2:L], in1=cur[:, :, lo2 - sh:L - sh], op=ALU.add),
                        reads=[curR], writes=[oR])
                    cur, curR = o[:, :, :], oR
                    lo = lo2
                    sh *= 2
                    bi ^= 1
                P.do("dve", lambda e, cur=cur, gi=gi, w=w: e.scalar_tensor_tensor(
                    out=mT[:, 2 * gi:2 * gi + 2, :], in0=cur[:, :, H:H + TT], scalar=1.0 / w,
                    in1=hf[:, 2 * gi:2 * gi + 2, H:H + TT], op0=ALU.mult, op1=ALU.subtract),
                    reads=[curR, hfR], writes=[mTR], nowaw=True)
                if first:
                    for k in range(2):
                        P.do("dve", lambda e, cur=cur, gi=gi, k=k: e.tensor_tensor(
                            out=tmp[:], in0=cur[:, k, H:2 * H], in1=rc[:, gi, :], op=ALU.mult),
                            reads=[curR, rcR], writes=[tmpR])
                        P.do("dve", lambda e, gi=gi, k=k: e.tensor_tensor(
                            out=mT[:, 2 * gi + k, 0:H], in0=tmp[:], in1=hf[:, 2 * gi + k, H:2 * H],
                            op=ALU.subtract), reads=[tmpR, hfR, mTR], writes=[mTR])
            if j + 1 < ntile:
                pre(j + 1)
            for oc in range(NC):
                gi, nh = oc // 2, oc % 2
                pb = 1 + oc % 2
                P.mm(pps[pb][:], ppsR[pb],
                     [(Wp[:, 2 * gi + k, nh * 128:(nh + 1) * 128], mT[:, 2 * gi + k, :]) for k in range(2)],
                     reads=[WpR, mTR])
                P.do("dve", lambda e, oc=oc, pb=pb, xt=xt: e.scalar_tensor_tensor(
                    out=xt[:, oc, :], in0=pps[pb][:], scalar=cx.pc("pool_scale", oc, 1), in1=xt[:, oc, :],
                    op0=ALU.mult, op1=ALU.add), reads=[ppsR[pb], xtR], writes=[xtR], nowaw=True)
            P.ld("sp", "o%d" % (j % 2), [(xv[:, :, sl], xt[:])], [], reads=[xtR])
        P.barrier()


def conv_phase(cx, xs, w_in, w_out):
    nc, P, NT, SEQ = cx.nc, cx.P, cx.NT, cx.SEQ
    H = 30
    KW = 31
    with ExitStack() as es:
        def sb(name, shape, dt):
            return es.enter_context(nc.sbuf_tensor(_u(name), shape, dt))

        ps, psR = mkps(cx, es)
        Win, WinR = load_w(cx, es, "cWin", w_in, D, 2 * D)
        Wout, WoutR = load_w(cx, es, "cWout", w_out, D, D)
        xt = [sb("xt%d" % i, [128, NC, TT], F32) for i in range(2)]
        xtR = [Reg(), Reg()]
        nb = NormBufs(cx, es)
        hT = sb("hT", [128, NC, TT], BF16)
        hTR = Reg()
        ident = sb("ident", [128, 128], F32)
        identR = Reg()
        onesf = sb("onesf", [128, 128], F32)
        Dg = sb("Dg", [128, NC * KW, 128], BF16)
        DgR = Reg()
        u = sb("u", [128, NC, H + TT], BF16)
        uR = Reg()
        sg = sb("sg", [128, TT], F32)
        sgR = Reg()
        v = sb("v", [128, NC, TT], F32)
        vR = Reg()
        zT = sb("zT", [128, NC, TT], BF16)
        zTR = Reg()
        mean = sb("mean", [128, TT], F32)
        meanR = Reg()
        var = sb("var", [128, TT], F32)
        varR = Reg()
        lneps = sb("lneps", [128, 1], F32)
        P.do("pool", lambda e: e.memset(lneps[:], LN_EPS), writes=[identR])
        P.do("pool", lambda e: e.memset(onesf[:], 1.0), writes=[identR])
        P.do("pool", lambda e: e.memset(ident[:], 1.0), writes=[identR])
        P.do("pool", lambda e: e.affine_select(out=ident[:], in_=ident[:], pattern=[[1, 128]],
                                               compare_op=ALU.is_equal, fill=0.0, base=0, channel_multiplier=-1),
             reads=[identR], writes=[identR])
        for k in range(KW):
            for c in range(NC):
                P.do("dve", lambda e, k=k, c=c: e.tensor_scalar(
                    out=Dg[:, c * KW + k, :], in0=ident[:], scalar1=cx.pc("conv_dw", k * 8 + c, 1), scalar2=None,
                    op0=ALU.mult), reads=[identR], writes=[DgR], nowaw=True)
        xv = xview(xs)
        tps = SEQ // TT
        ntile = NT // TT

        def pre(j):
            b = j % 2
            sl = slice(j * TT, (j + 1) * TT)
            P.ld("sp", "x%d" % b, [(xt[b][:], xv[:, :, sl])], [xtR[b]])
            rms_rstd(cx, nb, xt[b], xtR[b], ps[0], psR[0])
            rms_apply(cx, nb, xt[b], xtR[b], cx.pc("mix_norm_0"), hT, hTR)

        def inproj(j):
            first = (j % tps == 0)
            if first:
                P.do("pool", lambda e: e.memset(u[:, :, 0:H], 0.0), writes=[uR])
            else:
                P.do("act", lambda e: e.activation(out=u[:, :, 0:H], in_=u[:, :, TT:TT + H], func=AF.Copy),
                     reads=[uR], writes=[uR])
            for oc in range(NC):
                pa, pbk = 1 + (oc % 2), 3 + (oc % 2)
                P.mm(ps[pa][:], psR[pa], [(Win[:, c, oc * 128:(oc + 1) * 128], hT[:, c, :]) for c in range(NC)],
                     reads=[WinR, hTR])
                P.mm(ps[pbk][:], psR[pbk],
                     [(Win[:, c, D + oc * 128:D + (oc + 1) * 128], hT[:, c, :]) for c in range(NC)],
                     reads=[WinR, hTR])
                P.do("act", lambda e, oc=oc, pbk=pbk: e.activation(
                    out=sg[:], in_=ps[pbk][:], func=AF.Sigmoid, bias=cx.pc("conv_b_in", 8 + oc, 1)),
                    reads=[psR[pbk]], writes=[sgR])
                P.do("dve", lambda e, oc=oc, pa=pa: e.scalar_tensor_tensor(
                    out=u[:, oc, H:H + TT], in0=ps[pa][:], scalar=cx.pc("conv_b_in", oc, 1), in1=sg[:],
                    op0=ALU.add, op1=ALU.mult), reads=[psR[pa], sgR], writes=[uR], nowaw=True)

        def conv(j):
            for c in range(NC):
                pc_ = 5 + (c % 2)
                P.mm(ps[pc_][:], psR[pc_], [(Dg[:, c * KW + k, :], u[:, c, k:k + TT]) for k in range(KW)],
                     reads=[DgR, uR])
                P.do("act", lambda e, c=c, pc_=pc_: e.activation(
                    out=v[:, c, :], in_=ps[pc_][:], func=AF.Identity, bias=cx.pc("conv_dw_b", c, 1)),
                    reads=[psR[pc_]], writes=[vR], nowaw=True)
                P.do("act", lambda e, c=c, pc_=pc_: e.activation(
                    out=nb.sq[:, c, :], in_=ps[pc_][:], func=AF.Square, bias=cx.pc("conv_dw_b", c, 1)),
                    reads=[psR[pc_]], writes=[nb.sqR], nowaw=True)

        def rest_a(j):
            P.mm(ps[1][:], psR[1], [(onesf[:], v[:, c, :]) for c in range(NC)], reads=[vR, identR])
            P.mm(ps[2][:], psR[2], [(nb.ones[:], nb.sq[:, c, :]) for c in range(NC)], reads=[nb.sqR, nb.onesR])
            P.do("act", lambda e: e.activation(out=mean[:], in_=ps[1][:], func=AF.Copy, scale=1.0 / D),
                 reads=[psR[1]], writes=[meanR])
            P.do("dve", lambda e: e.tensor_tensor(out=var[:], in0=mean[:], in1=mean[:], op=ALU.mult),
                 reads=[meanR], writes=[varR])
            P.do("dve", lambda e: e.scalar_tensor_tensor(out=var[:], in0=ps[2][:], scalar=1.0 / D, in1=var[:],
                                                         op0=ALU.mult, op1=ALU.subtract),
                 reads=[psR[2], varR], writes=[varR])
            P.do("act", lambda e: e.activation(out=var[:], in_=var[:], func=AF.Ln, bias=lneps[:], scale=1.0),
                 reads=[varR, identR], writes=[varR])
            P.do("act", lambda e: e.activation(out=var[:], in_=var[:], func=AF.Exp, scale=-0.5),
                 reads=[varR], writes=[varR])
            for c in range(NC):
                P.do("dve", lambda e, c=c: e.tensor_tensor(out=v[:, c, :], in0=v[:, c, :], in1=mean[:],
                                                           op=ALU.subtract),
                     reads=[vR, meanR], writes=[vR], nowaw=True)
                P.do("dve", lambda e, c=c: e.tensor_tensor(out=v[:, c, :], in0=v[:, c, :], in1=var[:],
                                                           op=ALU.mult),
                     reads=[vR, varR], writes=[vR])
                P.do("act", lambda e, c=c: e.activation(
                    out=zT[:, c, :], in_=v[:, c, :], func=AF.Silu, scale=cx.pc("conv_ln_g", c, 1),
                    bias=cx.pc("conv_ln_b", c, 1)), reads=[vR], writes=[zTR], nowaw=True)

        def rest_b(j):
            b = j % 2
            sl = slice(j * TT, (j + 1) * TT)
            for oc in range(NC):
                pa = 3 + (oc % 2)
                P.mm(ps[pa][:], psR[pa], [(Wout[:, c, oc * 128:(oc + 1) * 128], zT[:, c, :]) for c in range(NC)],
                     reads=[WoutR, zTR])
                P.do("dve", lambda e, oc=oc, pa=pa: e.tensor_tensor(
                    out=xt[b][:, oc, :], in0=ps[pa][:], in1=xt[b][:, oc, :], op=ALU.add),
                    reads=[psR[pa], xtR[b]], writes=[xtR[b]], nowaw=True)
            P.ld("sp", "o%d" % b, [(xv[:, :, sl], xt[b][:])], [], reads=[xtR[b]])

        pre(0)
        for j in range(ntile):
            inproj(j)
            if j + 1 < ntile:
                pre(j + 1)
            conv(j)
            rest_a(j)
            rest_b(j)
        P.barrier()


NH = 16
DH = 64
KA_ROWS = 70


def fox_phase(cx, xs, w_in, w_out):
    nc, P, NT, SEQ = cx.nc, cx.P, cx.NT, cx.SEQ
    QA = nc.dram_tensor("fox_QA", [NH, KA_ROWS, NT], BF16).ap()
    KA = nc.dram_tensor("fox_KA", [NH, KA_ROWS, NT], BF16).ap()
    Vd = nc.dram_tensor("fox_V", [NT, D], BF16).ap()
    OTd = nc.dram_tensor("fox_OT", [D, NT], BF16).ap()
    xv = xview(xs)
    tps = SEQ // TT
    with ExitStack() as es:
        def sb(name, shape, dt):
            return es.enter_context(nc.sbuf_tensor(_u(name), shape, dt))

        ps, psR = mkps(cx, es, 6)
        Win, WinR = load_w(cx, es, "fWin", w_in, D, 3 * D + NH)
        xt2 = [sb("xt%d" % i, [128, NC, TT], F32) for i in range(2)]
        xt2R = [Reg(), Reg()]
        nb = NormBufs(cx, es)
        hT2 = [sb("hT%d" % i, [128, NC, TT], BF16) for i in range(2)]
        hT2R = [Reg(), Reg()]
        qk = [sb("qk%d" % i, [128, TT], BF16) for i in range(2)]
        qkR = [Reg(), Reg()]
        Vsb = sb("Vsb", [128, 4, D], BF16)
        VsbR = Reg()
        negb = sb("negb", [NH, 1], F32)
        negbR = Reg()
        e1 = sb("e1", [NH, TT], F32)
        e1R = Reg()
        onesf = sb("onesf", [NH, TT], F32)
        onesfR = Reg()
        cp = [sb("cp%d" % i, [NH, TT], F32) for i in range(2)]
        cpR = [Reg(), Reg()]
        r1 = sb("r1", [NH, TT], F32)
        r1R = Reg()
        spl = sb("spl", [NH, 6, TT], BF16)
        splR = Reg()
        ones3 = sb("ones3", [NH, 3, TT], BF16)
        ones3R = Reg()
        P.do("pool", lambda e: e.memset(onesf[:], 1.0), writes=[onesfR])
        P.do("pool", lambda e: e.memset(ones3[:], 1.0), writes=[ones3R])
        P.do("dve", lambda e: e.tensor_scalar(out=negb[:], in0=cx.pc("fox_b_f")[0:NH, :], scalar1=-1.0,
                                              scalar2=None, op0=ALU.mult), writes=[negbR])
        def pre_load(j):
            bb = j % 2
            P.ld("sp", "x%d" % bb, [(xt2[bb][:], xv[:, :, j * TT:(j + 1) * TT])], [xt2R[bb]])

        def pre(j):
            bb = j % 2
            rms_rstd(cx, nb, xt2[bb], xt2R[bb], ps[0], psR[0], skip_sq=True)
            rms_apply(cx, nb, xt2[bb], xt2R[bb], cx.pc("mix_norm_1"), hT2[bb], hT2R[bb])

        pre_load(0)
        rms_sq(cx, nb, xt2[0], xt2R[0])
        pre(0)
        for j in range(NT // TT):
            sl = slice(j * TT, (j + 1) * TT)
            first = (j % tps == 0)
            hT, hTR = hT2[j % 2], hT2R[j % 2]
            if j + 1 < NT // TT:
                pre_load(j + 1)
            for oc in range(16):
                if oc == 2 and j + 1 < NT // TT:
                    rms_sq(cx, nb, xt2[(j + 1) % 2], xt2R[(j + 1) % 2])
                if oc == 8 and j + 1 < NT // TT:
                    pre(j + 1)
                pb = 1 + oc % 2
                b = oc % 2
                P.mm(ps[pb][:], psR[pb], [(Win[:, c, oc * 128:(oc + 1) * 128], hT[:, c, :]) for c in range(NC)],
                     reads=[WinR, hTR])
                P.do("act", lambda e, pb=pb, b=b: e.activation(out=qk[b][:], in_=ps[pb][:], func=AF.Copy),
                     reads=[psR[pb]], writes=[qkR[b]])
                dst = QA if oc < 8 else KA
                h0 = 2 * (oc % 8)
                P.ld("sp", "qk%d" % b, [(dst[h0, 0:DH, sl], qk[b][0:DH, :]), (dst[h0 + 1, 0:DH, sl], qk[b][DH:128, :])],
                     [], reads=[qkR[b]])
            for tb in range(4):
                for nh in range(2):
                    pb = 3 + nh
                    P.mm(ps[pb][:], psR[pb],
                         [(hT[:, c, tb * 128:(tb + 1) * 128], Win[:, c, 2 * D + nh * 512:2 * D + (nh + 1) * 512])
                          for c in range(NC)], reads=[WinR, hTR])
                    P.do("dve", lambda e, pb=pb, tb=tb, nh=nh: e.tensor_copy(
                        out=Vsb[:, tb, nh * 512:(nh + 1) * 512], in_=ps[pb][:]),
                        reads=[psR[pb]], writes=[VsbR], nowaw=True)
            P.ld("sp", "vst", [(Vd[sl, :].rearrange("(tb p) f -> p tb f", p=128), Vsb[:])], [], reads=[VsbR])
            P.mm(ps[5][0:NH, :], psR[5], [(Win[:, c, 3 * D:3 * D + NH], hT[:, c, :]) for c in range(NC)],
                 reads=[WinR, hTR])
            P.do("act", lambda e: e.activation(out=e1[:], in_=ps[5][0:NH, :], func=AF.Exp, scale=-1.0,
                                               bias=negb[:]), reads=[psR[5], negbR], writes=[e1R])
            P.do("act", lambda e: e.activation(out=e1[:], in_=e1[:], func=AF.Ln, scale=1.0, bias=1.0),
                 reads=[e1R], writes=[e1R])
            b = j % 2
            if first:
                P.do("dve", lambda e, b=b: e.tensor_tensor_scan(out=cp[b][:], data0=onesf[:], data1=e1[:],
                                                                initial=0.0, op0=ALU.mult, op1=ALU.add),
                     reads=[onesfR, e1R], writes=[cpR[b]])
            else:
                P.do("dve", lambda e, b=b: e.tensor_tensor_scan(out=cp[b][:], data0=onesf[:], data1=e1[:],
                                                                initial=cp[1 - b][:, TT - 1:TT], op0=ALU.mult,
                                                                op1=ALU.add),
                     reads=[onesfR, e1R, cpR[1 - b]], writes=[cpR[b]])
            P.do("dve", lambda e, b=b: e.tensor_scalar(out=spl[:, 0, :], in0=cp[b][:], scalar1=8.0, scalar2=None,
                                                       op0=ALU.mult), reads=[cpR[b]], writes=[splR])
            P.do("dve", lambda e, b=b: e.scalar_tensor_tensor(out=r1[:], in0=cp[b][:], scalar=8.0, in1=spl[:, 0, :],
                                                              op0=ALU.mult, op1=ALU.subtract),
                 reads=[cpR[b], splR], writes=[r1R])
            P.do("dve", lambda e: e.tensor_copy(out=spl[:, 1, :], in_=r1[:]), reads=[r1R, splR], writes=[splR])
            P.do("dve", lambda e: e.tensor_tensor(out=r1[:], in0=r1[:], in1=spl[:, 1, :], op=ALU.subtract),
                 reads=[r1R, splR], writes=[r1R])
            P.do("dve", lambda e: e.tensor_copy(out=spl[:, 2, :], in_=r1[:]), reads=[r1R, splR], writes=[splR])
            P.do("dve", lambda e: e.tensor_scalar(out=spl[:, 3:6, :], in0=spl[:, 0:3, :], scalar1=-1.0, scalar2=None,
                                                  op0=ALU.mult), reads=[splR], writes=[splR])
            P.ld("sp", "spl", [(KA[:, DH:DH + 3, sl], spl[:, 0:3, :]), (QA[:, DH + 3:DH + 6, sl], spl[:, 3:6, :]),
                               (QA[:, DH:DH + 3, sl], ones3[:]), (KA[:, DH + 3:DH + 6, sl], ones3[:])],
                 [], reads=[splR, ones3R])
        P.barrier()
    nkt = SEQ // 128
    nqt = SEQ // TT
    with ExitStack() as es:
        def sb(name, shape, dt):
            return es.enter_context(nc.sbuf_tensor(_u(name), shape, dt))

        Qa = [sb("Qa%d" % i, [128, SEQ], BF16) for i in range(2)]
        Ka = [sb("Ka%d" % i, [128, SEQ], BF16) for i in range(2)]
        Va = [sb("Va%d" % i, [128, nkt, 128], BF16) for i in range(2)]
        QaR, KaR, VaR = [Reg(), Reg()], [Reg(), Reg()], [Reg(), Reg()]
        NPT = 3
        SK = 2
        psS = [es.enter_context(nc.psum_tensor(_u("psS"), [128, 2, TT], F32)) for _ in range(NPT)]
        psSR = [Reg() for _ in range(NPT)]
        psO = [es.enter_context(nc.psum_tensor(_u("psO"), [128, TT], F32)) for _ in range(2)]
        psOR = [Reg(), Reg()]
        Pt = [sb("Pt%d" % i, [128, 2, TT], BF16) for i in range(NPT)]
        PtR = [Reg() for _ in range(NPT)]
        rden = sb("rden", [128, TT], F32)
        rdenR = Reg()
        clp = sb("clp", [128, 2, TT], F32)
        clpR = Reg()
        rsh = sb("rsh", [DH, TT], F32)
        rshR = Reg()
        osb = [sb("osb%d" % i, [DH, TT], BF16) for i in range(2)]
        osbR = [Reg(), Reg()]
        for i in range(2):
            P.do("pool", lambda e, i=i: e.memset(Va[i][:, :, DH:128], 1.0), writes=[VaR[i]])
            P.do("pool", lambda e, i=i: e.memset(Qa[i][64:128, :], 0.0), writes=[QaR[i]])
            P.do("pool", lambda e, i=i: e.memset(Ka[i][64:128, :], 0.0), writes=[KaR[i]])
        nseq = NT // SEQ
        pairs = [(s, h) for s in range(nseq) for h in range(NH)]

        def load(ip):
            s, h = pairs[ip]
            b = ip % 2
            tsl = slice(s * SEQ, (s + 1) * SEQ)
            P.ld("sp", "qa%d" % b, [(Qa[b][0:KA_ROWS, :], QA[h, :, tsl])], [QaR[b]])
            P.ld("sp", "ka%d" % b, [(Ka[b][0:KA_ROWS, :], KA[h, :, tsl])], [KaR[b]])
            P.ld("sp", "va%d" % b,
                 [(Va[b][:, :, 0:DH], Vd[tsl, h * DH:(h + 1) * DH].rearrange("(kt p) f -> p kt f", p=128))],
                 [VaR[b]])

        nmask = [sb("nmask%d" % m, [128, 2, TT], BF16) for m in range(2)]
        nmaskR = Reg()
        for m in range(2):
            P.do("pool", lambda e, m=m: e.memset(nmask[m][:], 0.0), writes=[nmaskR])
            P.do("pool", lambda e, m=m: e.affine_select(
                out=nmask[m][:], in_=nmask[m][:], pattern=[[-128, 2], [1, TT]], compare_op=ALU.is_ge, fill=-30000.0,
                base=-256 * m, channel_multiplier=-1), reads=[nmaskR], writes=[nmaskR])
        idf = sb("idf", [128, 128], F32)
        identb = sb("identb", [128, 128], BF16)
        P.do("pool", lambda e: e.memset(idf[:], 1.0), writes=[nmaskR])
        P.do("pool", lambda e: e.affine_select(out=idf[:], in_=idf[:], pattern=[[1, 128]], compare_op=ALU.is_equal,
                                               fill=0.0, base=0, channel_multiplier=-1),
             reads=[nmaskR], writes=[nmaskR])
        P.do("pool", lambda e: e.tensor_copy(out=identb[:], in_=idf[:]), reads=[nmaskR], writes=[nmaskR])
        steps = []
        for ip in range(len(pairs)):
            for qi in range(nqt):
                for kp in range(2 * (qi + 1)):
                    steps.append((ip, qi, kp))

        def s_stage(t):
            ip, qi, kp = steps[t]
            b = ip % 2
            r = t % NPT
            qs = slice(qi * TT, (qi + 1) * TT)
            diag = kp >= 2 * qi
            m = kp - 2 * qi
            for a in range(2):
                ki = 2 * kp + a
                items = [(Ka[b][:, ki * 128:(ki + 1) * 128], Qa[b][:, qs])]
                if diag:
                    items.append((identb[:], nmask[m][:, a, :]))
                P.mm(psS[r][:, a, :], psSR[r], items, reads=[KaR[b], QaR[b], nmaskR])
            P.do("act", lambda e, r=r: e.activation(out=Pt[r][:], in_=psS[r][:], func=AF.Exp, scale=0.125),
                 reads=[psSR[r]], writes=[PtR[r]])

        qcount = [0]

        def pv_stage(t):
            ip, qi, kp = steps[t]
            s_, h = pairs[ip]
            b = ip % 2
            r = t % NPT
            npair = 2 * (qi + 1)
            ob = (ip * nqt + qi) % 2
            for a in range(2):
                ki = 2 * kp + a
                fst = (ki == 0)
                lst = (ki == 2 * npair - 1)
                P.op("pe", lambda e, r=r, ki=ki, a=a, ob=ob, b=b, fst=fst, lst=lst: e.matmul(
                    psO[ob][:], lhsT=Va[b][:, ki, :], rhs=Pt[r][:, a, :], start=fst, stop=lst),
                    P._deps("pe", [VaR[b], PtR[r]], [psOR[ob]] if fst else [], ()), sig=True)
                tok = P.last["pe"]
                _merge(PtR[r].r, tok)
                _merge(VaR[b].r, tok)
                if lst:
                    psOR[ob].w = {"pe": tok}
                    psOR[ob].r = {}
            if kp == npair - 1:
                P.do("dve", lambda e, ob=ob: e.reciprocal(out=rden[DH:128, :], in_=psO[ob][DH:128, :]),
                     reads=[psOR[ob]], writes=[rdenR])
                P.do("dve", lambda e: e.tensor_copy(out=rsh[:], in_=rden[DH:128, :]), reads=[rdenR], writes=[rshR])
                P.do("dve", lambda e, ob=ob: e.tensor_tensor(out=osb[ob][:], in0=psO[ob][0:DH, :], in1=rsh[:],
                                                             op=ALU.mult),
                     reads=[psOR[ob], rshR], writes=[osbR[ob]])
                P.ld("sp", "ot%d" % ob, [(OTd[h * DH:(h + 1) * DH, s_ * SEQ + qi * TT:s_ * SEQ + (qi + 1) * TT],
                                          osb[ob][:])], [], reads=[osbR[ob]])

        load(0)
        if len(pairs) > 1:
            load(1)

        def pv_and_prefetch(t):
            pv_stage(t)
            ip, qi, kp = steps[t]
            if qi == nqt - 1 and kp == 2 * nqt - 1 and ip + 2 < len(pairs):
                load(ip + 2)

        for t in range(len(steps)):
            s_stage(t)
            if t >= SK:
                pv_and_prefetch(t - SK)
        for t in range(max(0, len(steps) - SK), len(steps)):
            pv_and_prefetch(t)
        P.barrier()
    with ExitStack() as es:
        def sb(name, shape, dt):
            return es.enter_context(nc.sbuf_tensor(_u(name), shape, dt))

        psc, pscR = mkps(cx, es, 2)
        Wout, WoutR = load_w(cx, es, "fWout", w_out, D, D)
        xt = [sb("xt%d" % i, [128, NC, TT], F32) for i in range(2)]
        xtR = [Reg(), Reg()]
        ot = [sb("ot%d" % i, [128, NC, TT], BF16) for i in range(2)]
        otR = [Reg(), Reg()]
        otv = xview(OTd)
        for j in range(NT // TT):
            b = j % 2
            sl = slice(j * TT, (j + 1) * TT)
            P.ld("sp", "x%d" % b, [(xt[b][:], xv[:, :, sl])], [xtR[b]])
            P.ld("sp", "otl%d" % b, [(ot[b][:], otv[:, :, sl])], [otR[b]])
            for oc in range(NC):
                pb = oc % 2
                P.mm(psc[pb][:], pscR[pb], [(Wout[:, c, oc * 128:(oc + 1) * 128], ot[b][:, c, :]) for c in range(NC)],
                     reads=[WoutR, otR[b]])
                P.do("dve", lambda e, oc=oc, pb=pb, b=b: e.tensor_tensor(
                    out=xt[b][:, oc, :], in0=psc[pb][:], in1=xt[b][:, oc, :], op=ALU.add),
                    reads=[pscR[pb], xtR[b]], writes=[xtR[b]], nowaw=True)
            P.ld("sp", "o%d" % b, [(xv[:, :, sl], xt[b][:])], [], reads=[xtR[b]])
        P.barrier()


HH = 8
CH = 64


def hgrn_phase(cx, xs, w_in, w_out, layer=2):
    nc, P, NT, SEQ = cx.nc, cx.P, cx.NT, cx.SEQ
    xv = xview(xs)
    tps = SEQ // TT
    NCK = TT // CH
    with ExitStack() as es:
        ps, psR = mkps(cx, es, 7)
        def sb(name, shape, dt):
            return es.enter_context(nc.sbuf_tensor(_u(name), shape, dt))

        Win, WinR = load_w(cx, es, "hWin", w_in, D, 4 * D)
        Wout, WoutR = load_w(cx, es, "hWout", w_out, D, D)
        psT = es.enter_context(nc.psum_tensor(_u("psT"), [128, 2, TT], BF16))
        psTR = [Reg()]
        xt = sb("xt", [128, NC, TT], F32)
        xtR = Reg()
        nb = NormBufs(cx, es)
        hT = sb("hT", [128, NC, TT], BF16)
        hTR = Reg()
        Vsb = sb("Vsb", [128, 4, D], BF16)
        VsbR = Reg()
        F1 = sb("F1", [128, HH, TT], F32)
        F1R = Reg()
        G = sb("G", [128, HH, TT], F32)
        GR = Reg()
        QgT = sb("QgT", [128, HH, TT], BF16)
        QgTR = Reg()
        KdT = sb("KdT", [128, HH, TT], BF16)
        KdTR = Reg()
        Kd = sb("Kd", [128, HH, 4, 128], BF16)
        KdR = Reg()
        eGl = sb("eGl", [128, HH, NCK], F32)
        eGlR = Reg()
        eGp = sb("eGp", [128, HH], F32)
        eGpR = Reg()
        qs0 = sb("qs0", [128, TT], F32)
        qs = [qs0, qs0]
        qs0R = Reg()
        qsR = [qs0R, qs0R]
        cmask = sb("cmask", [128, TT], F32)
        cmaskR = Reg()
        m64 = sb("m64", [128, CH], F32)
        m64R = Reg()
        identb = sb("identb", [128, 128], BF16)
        identR = Reg()
        At = sb("At", [128, HH, CH], BF16)
        AtR = Reg()
        m64x = sb("m64x", [128, HH, CH], F32)
        U = sb("U", [128, HH, 128], F32)
        UR = [Reg() for _ in range(HH)]
        Sbf = sb("Sbf", [128, HH, 128], BF16)
        SbfR = [Reg() for _ in range(HH)]
        zT = sb("zT", [128, HH, TT], BF16)
        zTR = Reg()
        rs4 = [sb("rs%d" % i, [128, TT], F32) for i in range(2)]
        rs4R = [Reg() for _ in range(2)]
        lbt = sb("lbt", [128, 40], F32)
        lbR = Reg()
        lb = sb("lb", [128, HH], F32)
        oml = sb("oml", [128, HH], F32)
        gs = G[:].bitcast(BF16) if hasattr(G[:], "bitcast") else None
        P.do("pool", lambda e: e.memset(cmask[:], 1.0), writes=[cmaskR])
        cm3 = cmask[:].rearrange("p (c t) -> p c t", t=CH)
        P.do("pool", lambda e: e.memset(cm3[:, :, 0:1], 0.0), reads=[cmaskR], writes=[cmaskR])
        P.do("pool", lambda e: e.memset(m64x[:], 1.0), writes=[m64R])
        for hb in range(2):
            P.do("pool", lambda e, hb=hb: e.affine_select(
                out=m64x[hb * 64:(hb + 1) * 64, :, :], in_=m64x[hb * 64:(hb + 1) * 64, :, :],
                pattern=[[0, HH], [1, CH]], compare_op=ALU.is_ge, fill=0.0, base=0, channel_multiplier=-1),
                reads=[m64R], writes=[m64R])
        idf = sb("idf", [128, 128], F32)
        P.do("pool", lambda e: e.memset(idf[:], 1.0), writes=[identR])
        P.do("pool", lambda e: e.affine_select(out=idf[:], in_=idf[:], pattern=[[1, 128]], compare_op=ALU.is_equal,
                                               fill=0.0, base=0, channel_multiplier=-1),
             reads=[identR], writes=[identR])
        P.do("pool", lambda e: e.tensor_copy(out=identb[:], in_=idf[:]), reads=[identR], writes=[identR])
        P.do("act", lambda e: e.activation(out=lbt[:, 0:32], in_=cx.pc("hgrn_lb"), func=AF.Exp), writes=[lbR])
        P.do("dve", lambda e: e.tensor_tensor(out=lbt[:, 32:40], in0=lbt[:, 0:8], in1=lbt[:, 8:16], op=ALU.add),
             reads=[lbR], writes=[lbR])
        P.do("dve", lambda e: e.tensor_tensor(out=lbt[:, 32:40], in0=lbt[:, 32:40], in1=lbt[:, 16:24], op=ALU.add),
             reads=[lbR], writes=[lbR])
        P.do("dve", lambda e: e.tensor_tensor(out=lbt[:, 32:40], in0=lbt[:, 32:40], in1=lbt[:, 24:32], op=ALU.add),
             reads=[lbR], writes=[lbR])
        P.do("dve", lambda e: e.reciprocal(out=lbt[:, 32:40], in_=lbt[:, 32:40]), reads=[lbR], writes=[lbR])
        P.do("dve", lambda e: e.tensor_copy(out=lb[:], in_=lbt[:, 8:16]), reads=[lbR], writes=[lbR])
        for l in range(2, layer + 1):
            P.do("dve", lambda e, l=l: e.tensor_tensor(out=lb[:], in0=lb[:], in1=lbt[:, 8 * l:8 * l + 8], op=ALU.add),
                 reads=[lbR], writes=[lbR])
        P.do("dve", lambda e: e.tensor_tensor(out=lb[:], in0=lb[:], in1=lbt[:, 32:40], op=ALU.mult),
             reads=[lbR], writes=[lbR])
        P.do("dve", lambda e: e.tensor_scalar(out=oml[:], in0=lb[:], scalar1=-1.0, scalar2=1.0, op0=ALU.mult,
                                              op1=ALU.add), reads=[lbR], writes=[lbR])
        osb = F1
        psA, psAR = ps[3], Reg()
        psSt = [ps[4], ps[5]]
        psStR = [Reg(), Reg()]
        psO, psOR = ps[6], Reg()
        Gv = G[:].rearrange("p h (c t) -> p h c t", t=CH)

        for j in range(NT // TT):
            sl = slice(j * TT, (j + 1) * TT)
            first = (j % tps == 0)
            P.ld("sp", "x0", [(xt[:], xv[:, :, sl])], [xtR])
            rms_rstd(cx, nb, xt, xtR, ps[0], psR[0])
            rms_apply(cx, nb, xt, xtR, cx.pc("mix_norm_%d" % layer), hT, hTR)
            for tb in range(4):
                for nh in range(2):
                    pb = 1 + nh
                    P.mm(ps[pb][:], psR[pb],
                         [(hT[:, c, tb * 128:(tb + 1) * 128], Win[:, c, 2 * D + nh * 512:2 * D + (nh + 1) * 512])
                          for c in range(NC)], reads=[WinR, hTR])
                    P.do("dve", lambda e, pb=pb, tb=tb, nh=nh: e.tensor_copy(
                        out=Vsb[:, tb, nh * 512:(nh + 1) * 512], in_=ps[pb][:]),
                        reads=[psR[pb]], writes=[VsbR], nowaw=True)
            for h in range(HH):
                pb = 1 + h % 2
                P.mm(ps[pb][:], psR[pb], [(Win[:, c, D + h * 128:D + (h + 1) * 128], hT[:, c, :]) for c in range(NC)],
                     reads=[WinR, hTR])
                P.do("act", lambda e, pb=pb, h=h: e.activation(out=F1[:, h, :], in_=ps[pb][:], func=AF.Sigmoid),
                     reads=[psR[pb]], writes=[F1R], nowaw=True)
            for h in range(HH):
                pb = 1 + h % 2
                P.mm(ps[pb][:], psR[pb], [(Win[:, c, h * 128:(h + 1) * 128], hT[:, c, :]) for c in range(NC)],
                     reads=[WinR, hTR])
                P.do("act", lambda e, pb=pb, h=h: e.activation(out=QgT[:, h, :], in_=ps[pb][:], func=AF.Silu),
                     reads=[psR[pb]], writes=[QgTR], nowaw=(h > 0))
            for h in range(HH):
                P.do("dve", lambda e, h=h: e.tensor_scalar(out=F1[:, h, :], in0=F1[:, h, :], scalar1=oml[:, h:h + 1],
                                                           scalar2=lb[:, h:h + 1], op0=ALU.mult, op1=ALU.add),
                     reads=[F1R, lbR], writes=[F1R], nowaw=(h > 0))
            P.do("act", lambda e: e.activation(out=G[:], in_=F1[:], func=AF.Ln), reads=[F1R], writes=[GR])
            P.do("dve", lambda e: e.tensor_scalar(out=F1[:], in0=F1[:], scalar1=-1.0, scalar2=1.0, op0=ALU.mult,
                                                  op1=ALU.add), reads=[F1R], writes=[F1R])
            for h in range(HH):
                P.do("dve", lambda e, h=h: e.tensor_tensor_scan(out=G[:, h, :], data0=cmask[:], data1=G[:, h, :],
                                                                initial=0.0, op0=ALU.mult, op1=ALU.add),
                     reads=[GR, cmaskR], writes=[GR], nowaw=(h > 0))
            if not first:
                P.do("act", lambda e: e.activation(out=eGp[:], in_=eGl[:, :, NCK - 1], func=AF.Copy),
                     reads=[eGlR], writes=[eGpR])
            else:
                P.do("pool", lambda e: e.memset(eGp[:], 1.0), writes=[eGpR])
                P.do("pool", lambda e: e.memset(U[:], 0.0), writes=UR)
                P.do("pool", lambda e: e.memset(Sbf[:], 0.0), writes=SbfR)
            P.do("act", lambda e: e.activation(out=zT[:], in_=G[:], func=AF.Exp), reads=[GR], writes=[zTR])
            P.do("act", lambda e: e.activation(out=KdT[:], in_=G[:], func=AF.Exp, scale=-1.0),
                 reads=[GR], writes=[KdTR])
            P.do("act", lambda e: e.activation(out=eGl[:], in_=Gv[:, :, :, CH - 1], func=AF.Exp),
                 reads=[GR], writes=[eGlR])
            P.do("dve", lambda e: e.tensor_tensor(out=QgT[:], in0=QgT[:], in1=zT[:], op=ALU.mult),
                 reads=[QgTR, zTR], writes=[QgTR])
            P.do("dve", lambda e: e.tensor_tensor(out=KdT[:], in0=F1[:], in1=KdT[:], op=ALU.mult),
                 reads=[F1R, KdTR], writes=[KdTR])
            gsv = G[:].bitcast(BF16)[:, 0:HH * TT].rearrange("p (h t) -> p h t", t=TT) if False else None
            for h in range(HH):
                pb = 1 + h % 2
                P.mm(ps[pb][:], psR[pb],
                     [(Win[:, c, 3 * D + h * 128:3 * D + (h + 1) * 128], hT[:, c, :]) for c in range(NC)],
                     reads=[WinR, hTR])
                P.do("act", lambda e, pb=pb, h=h: e.activation(out=G[:, h, :], in_=ps[pb][:], func=AF.Silu),
                     reads=[psR[pb]], writes=[GR], nowaw=(h > 0))
            for hp in range(HH // 2):
                deps = P._deps("pe", [KdTR, identR], [psTR[0]], ())
                tok = None
                n = 0
                for hb in range(2):
                    h = 2 * hp + hb
                    for blk in range(4):
                        tok = P.op("pe", lambda e, h=h, hb=hb, blk=blk: e.transpose(
                            psT[:, hb, blk * 128:(blk + 1) * 128], KdT[:, h, blk * 128:(blk + 1) * 128], identb[:]),
                            deps if n == 0 else (), sig=(n == 7))
                        n += 1
                P._mark(tok, [KdTR, identR], [psTR[0]])
                P.do("act", lambda e, hp=hp: e.activation(
                    out=Kd[:, 2 * hp:2 * hp + 2, :, :],
                    in_=psT[:].rearrange("p a (b k) -> p a b k", k=128), func=AF.Copy),
                    reads=[psTR[0]], writes=[KdR], nowaw=(hp > 0))
            for ch in range(NCK):
                base = (ch % 2) * 64
                blk = ch // 2
                cols = slice(ch * CH, (ch + 1) * CH)
                rows = slice(base, base + 64)
                for h in range(HH):
                    P.mm(psA[rows, h * CH:(h + 1) * CH], psAR, [(KdT[:, h, cols], QgT[:, h, cols])],
                         reads=[KdTR, QgTR])
                P.do("dve", lambda e, rows=rows: e.tensor_tensor(
                    out=At[rows, :, :], in0=psA[rows, :].rearrange("p (h t) -> p h t", t=CH), in1=m64x[rows, :, :],
                    op=ALU.mult), reads=[psAR, m64R], writes=[AtR])
                for h in range(HH):
                    vcols = slice(h * 128, (h + 1) * 128)
                    P.mm(psSt[h // 4][:, (h % 4) * 128:(h % 4 + 1) * 128], psStR[h // 4],
                         [(Kd[rows, h, blk, :], Vsb[rows, blk, vcols])], reads=[KdR, VsbR])
                for h in range(HH):
                    vcols = slice(h * 128, (h + 1) * 128)
                    P.mm(psO[:, h * CH:(h + 1) * CH], psOR,
                         [(Sbf[:, h, :], QgT[:, h, cols]), (Vsb[rows, blk, vcols], At[rows, h, :])],
                         reads=[SbfR[h], QgTR, VsbR, AtR])
                for h in range(HH):
                    sc = eGp[:, h:h + 1] if ch == 0 else eGl[:, h, ch - 1:ch]
                    P.do("dve", lambda e, h=h, sc=sc: e.scalar_tensor_tensor(
                        out=U[:, h, :], in0=U[:, h, :], scalar=sc, in1=psSt[h // 4][:, (h % 4) * 128:(h % 4 + 1) * 128],
                        op0=ALU.mult, op1=ALU.add), reads=[UR[h], psStR[h // 4], eGlR, eGpR], writes=[UR[h]])
                    P.do("act", lambda e, h=h, ch=ch: e.activation(out=Sbf[:, h, :], in_=U[:, h, :], func=AF.Copy,
                                                                   scale=eGl[:, h, ch:ch + 1]),
                         reads=[UR[h], eGlR], writes=[SbfR[h]])
                P.do("act", lambda e, cols=cols: e.activation(
                    out=osb[:, :, cols], in_=psO[:].rearrange("p (h t) -> p h t", t=CH), func=AF.Copy),
                    reads=[psOR, KdTR], writes=[F1R], nowaw=(ch > 0))
            P.do("act", lambda e: e.activation(out=nb.sq[:], in_=osb[:], func=AF.Square), reads=[F1R],
                 writes=[nb.sqR])
            pbanks = [0, 3, 4, 5]

            def post_mm(h):
                pb = pbanks[h % 4]
                P.mm(ps[pb][:], psR[pb], [(nb.ones[:], nb.sq[:, h, :])], reads=[nb.sqR, nb.onesR])

            for h in range(4):
                post_mm(h)
            for h in range(HH):
                pb = pbanks[h % 4]
                rb = h % 2
                P.do("act", lambda e, pb=pb, rb=rb: e.activation(out=rs4[rb][:], in_=ps[pb][:], func=AF.Ln,
                                                                 bias=nb.eps[:], scale=1.0 / 128),
                     reads=[psR[pb], nb.onesR], writes=[rs4R[rb]])
                P.do("act", lambda e, rb=rb: e.activation(out=rs4[rb][:], in_=rs4[rb][:], func=AF.Exp, scale=-0.5),
                     reads=[rs4R[rb]], writes=[rs4R[rb]])
                if h + 4 < HH:
                    post_mm(h + 4)
                P.do("dve", lambda e, h=h, rb=rb: e.tensor_tensor(out=osb[:, h, :], in0=osb[:, h, :], in1=rs4[rb][:],
                                                                  op=ALU.mult),
                     reads=[F1R, rs4R[rb]], writes=[F1R], nowaw=(h > 0))
                P.do("dve", lambda e, h=h: e.scalar_tensor_tensor(
                    out=zT[:, h, :], in0=osb[:, h, :], scalar=cx.pc("hgrn_norm", h, 1), in1=G[:, h, :],
                    op0=ALU.mult, op1=ALU.mult), reads=[F1R, GR], writes=[zTR], nowaw=(h > 0))
            for oc in range(NC):
                pb = 1 + oc % 2
                P.mm(ps[pb][:], psR[pb], [(Wout[:, h, oc * 128:(oc + 1) * 128], zT[:, h, :]) for h in range(HH)],
                     reads=[WoutR, zTR])
                P.do("dve", lambda e, oc=oc, pb=pb: e.tensor_tensor(
                    out=xt[:, oc, :], in0=ps[pb][:], in1=xt[:, oc, :], op=ALU.add),
                    reads=[psR[pb], xtR], writes=[xtR], nowaw=True)
            P.ld("sp", "o0", [(xv[:, :, sl], xt[:])], [], reads=[xtR])
        P.barrier()


def build_test(NT, SEQ, phase, dram_specs, ins_order=None):
    nc = bass.Bass("TRN2", target_bir_lowering=False)
    T = {}
    T["xT"] = nc.dram_tensor("xT", [D, NT], F32, kind="ExternalInput").ap()
    T["par"] = nc.dram_tensor("par", [128, NPAR], F32, kind="ExternalInput").ap()
    for k, shp in dram_specs.items():
        T[k] = nc.dram_tensor(k, list(shp), F32, kind="ExternalInput").ap()
    T["yT"] = nc.dram_tensor("yT", [D, NT], F32, kind="ExternalOutput").ap()
    P = Prog(nc)
    cx = Ctx(nc, P, NT, SEQ)
    with ExitStack() as es:
        par_sb = es.enter_context(nc.sbuf_tensor("par_sb", [128, NPAR], F32))
        cx.par = par_sb
        P.dma("sp", "par", par_sb[:], T["par"][:, :])
        P.barrier()
        phase(cx, T)
        P.barrier()
        P.emit()
    return nc


W_SPECS = {
    "ffn_w_gate": (4, 2, D, DFF), "ffn_w_up": (4, 2, D, DFF), "ffn_w_down": (4, 2, DFF, D),
    "conv_w_in": (1, D, 2 * D), "conv_w_out": (1, D, D),
    "fox_w_in": (1, D, 3 * D + NH), "fox_w_out": (1, D, D),
    "hgrn_w_in": (1, D, 4 * D), "hgrn_w_out": (1, D, D),
    "pool_w": (1, 4, 256, 256),
}


def build_full(NT=8192, SEQ=4096, depth=4):
    nc = bass.Bass("TRN2", target_bir_lowering=False)
    T = {}
    T["xT"] = nc.dram_tensor("xT", [D, NT], F32, kind="ExternalInput").ap()
    T["par"] = nc.dram_tensor("par", [128, NPAR], F32, kind="ExternalInput").ap()
    for k, shp in W_SPECS.items():
        T[k] = nc.dram_tensor(k, list(shp), F32, kind="ExternalInput").ap()
    yT = nc.dram_tensor("yT", [D, NT], F32, kind="ExternalOutput").ap()
    P = Prog(nc)
    cx = Ctx(nc, P, NT, SEQ)
    with ExitStack() as es:
        par_sb = es.enter_context(nc.sbuf_tensor("par_sb", [128, NPAR], F32))
        cx.par = par_sb
        P.dma("sp", "par", par_sb[:], T["par"][:, :])
        P.barrier()
        for i in range(depth):
            src = T["xT"] if i == 0 else yT
            ffn_phase(cx, src, yT, T["ffn_w_gate"][i, 0], T["ffn_w_up"][i, 0], T["ffn_w_down"][i, 0],
                      cx.pc("ffn_norm_%d_0" % i))
            m = i % 4
            if m == 0:
                conv_phase(cx, yT, T["conv_w_in"][0], T["conv_w_out"][0])
            elif m == 1:
                fox_phase(cx, yT, T["fox_w_in"][0], T["fox_w_out"][0])
            elif m == 2:
                hgrn_phase(cx, yT, T["hgrn_w_in"][0], T["hgrn_w_out"][0], layer=i)
            else:
                pool_phase(cx, yT, T["pool_w"][0])
            ffn_phase(cx, yT, yT, T["ffn_w_gate"][i, 1], T["ffn_w_up"][i, 1], T["ffn_w_down"][i, 1],
                      cx.pc("ffn_norm_%d_1" % i), final_g=(cx.pc("final_norm") if i == depth - 1 else None))
        P.barrier()
        P.emit()
    return nc


_NC_CACHE = {}


def kernel(**inputs):
    x = np.asarray(inputs["x"], dtype=np.float32)
    B, S, Dm = x.shape
    bpc = B // NCORES
    NT = bpc * S
    key = (NT, S)
    if key not in _NC_CACHE:
        _NC_CACHE[key] = build_full(NT, S)
    nc = _NC_CACHE[key]
    par = pack_params(inputs)
    ws = {k: np.ascontiguousarray(np.asarray(inputs[k], dtype=np.float32)) for k in W_SPECS}
    in_maps = []
    for c in range(NCORES):
        xc = x[c * bpc:(c + 1) * bpc].reshape(NT, Dm)
        m = {"xT": np.ascontiguousarray(xc.T), "par": par}
        m.update(ws)
        in_maps.append(m)
    res = run_bass_kernel_spmd(nc, in_maps, core_ids=list(range(NCORES)))
    outs = [np.asarray(r["yT"]).T.reshape(bpc, S, Dm) for r in res.results]
    return np.ascontiguousarray(np.concatenate(outs, axis=0)).astype(np.float32)
```
